# Optimizing a Trainium2 kernel written in Bass

```python
import math
import jax, jax.numpy as jnp
from jax import lax
import numpy as np

D_MODEL = 1024
BATCH = 8
SEQ = 4096
DEPTH = 2

HEAD_DIM = 64
ATTN_GROUPS = ((128, 1), (512, 4), (2048, 16))
HEADS_PER_GROUP = 4
N_ATTN_HEADS = HEADS_PER_GROUP * len(ATTN_GROUPS)
ATTN_WIDTH = N_ATTN_HEADS * HEAD_DIM
ATTN_OUT_WIDTH = HEADS_PER_GROUP * HEAD_DIM
ROT_DIM = HEAD_DIM // 4
ROPE_THETA = 500000.0
POOL_WINDOWS = (2, 4, 8, 16)
POOL_WIDTH = D_MODEL // 2
POOL_GROUP = POOL_WIDTH // len(POOL_WINDOWS)
N_BRANCH = 2
IN_WIDTH = 3 * ATTN_WIDTH + POOL_WIDTH + N_BRANCH * D_MODEL
D_FF = (8 * D_MODEL // 3 + 127) // 128 * 128
CONV_WIDTH = 3
PLE_DIM = 256
DN_ALPHA = (2.0 * DEPTH) ** 0.25
DN_BETA = (8.0 * DEPTH) ** -0.25
LN_EPS = 1e-5
NEG_INF = -1e30

kernel_name = "hybrid_dilated_attn_pool_encoder"


def layer_norm(x, g, b):
    xf = x.astype(jnp.float32)
    mu = jnp.mean(xf, axis=-1, keepdims=True)
    var = jnp.mean(jnp.square(xf - mu), axis=-1, keepdims=True)
    y = (xf - mu) * lax.rsqrt(var + LN_EPS)
    return (y * g.astype(jnp.float32) + b.astype(jnp.float32)).astype(x.dtype)


def rope_tables(seq_len):
    pos = jnp.arange(seq_len, dtype=jnp.float32)
    inv_freq = ROPE_THETA ** (-jnp.arange(0, ROT_DIM, 2, dtype=jnp.float32) / ROT_DIM)
    ang = pos[:, None] * inv_freq[None, :]
    return jnp.cos(ang), jnp.sin(ang)


def apply_partial_rope(t, cos, sin):
    half = ROT_DIM // 2
    c = cos[None, :, None, :].astype(t.dtype)
    s = sin[None, :, None, :].astype(t.dtype)
    t1 = t[..., :half]
    t2 = t[..., half:ROT_DIM]
    return jnp.concatenate([t1 * c - t2 * s, t2 * c + t1 * s, t[..., ROT_DIM:]], axis=-1)


def dilated_window_attention(q, k, v, dil, radius):
    B, S, H, E = q.shape
    L = S // dil
    nblk = -(-L // radius)
    Lp = nblk * radius
    scale = 1.0 / math.sqrt(E)

    def sub(t):
        return t.reshape(B, L, dil, H, E).transpose(0, 2, 1, 3, 4)

    qb = jnp.pad(sub(q), ((0, 0), (0, 0), (0, Lp - L), (0, 0), (0, 0))).reshape(B, dil, nblk, radius, H, E)

    def window(t):
        tp = jnp.pad(sub(t), ((0, 0), (0, 0), (radius, Lp - L + radius), (0, 0), (0, 0)))
        tp = tp.reshape(B, dil, nblk + 2, radius, H, E)
        return jnp.concatenate([tp[:, :, :-2], tp[:, :, 1:-1], tp[:, :, 2:]], axis=3)

    kw = window(k)
    vw = window(v)
    blk = jnp.arange(nblk)[:, None] * radius
    qpos = blk + jnp.arange(radius)[None, :]
    kpos = blk - radius + jnp.arange(3 * radius)[None, :]
    valid = ((jnp.abs(qpos[:, :, None] - kpos[:, None, :]) <= radius)
             & (kpos >= 0)[:, None, :] & (kpos < L)[:, None, :])

    s = jnp.einsum('brnqhe,brnkhe->brnhqk', qb, kw).astype(jnp.float32) * scale
    s = jnp.where(valid[:, None, :, :], s, NEG_INF)
    lse = jax.nn.logsumexp(s, axis=-1)
    prob = jnp.exp(s - lse[..., None]).astype(v.dtype)
    o = jnp.einsum('brnhqk,brnkhe->brnqhe', prob, vw)
    o = o.reshape(B, dil, Lp, H, E)[:, :, :L].transpose(0, 2, 1, 3, 4).reshape(B, S, H, E)
    lse = lse.transpose(0, 1, 2, 4, 3).reshape(B, dil, Lp, H)[:, :, :L]
    lse = lse.transpose(0, 2, 1, 3).reshape(B, S, H)
    return o, lse


def dilated_attention_branch(q, k, v):
    B, S = q.shape[:2]
    outs, lses = [], []
    for gi, (win, dil) in enumerate(ATTN_GROUPS):
        sl = slice(gi * HEADS_PER_GROUP, (gi + 1) * HEADS_PER_GROUP)
        o, l = dilated_window_attention(q[:, :, sl], k[:, :, sl], v[:, :, sl], dil, win // (2 * dil))
        outs.append(o)
        lses.append(l)
    wts = jax.nn.softmax(jnp.stack(lses, axis=0), axis=0)
    o = jnp.einsum('gbsh,gbshe->bshe', wts, jnp.stack(outs, axis=0).astype(jnp.float32))
    return o.astype(q.dtype).reshape(B, S, ATTN_OUT_WIDTH)


def multiscale_pool_branch(c, w_pool, pool_scale):
    S = c.shape[1]
    cf = c.astype(jnp.float32)
    cs = jnp.pad(jnp.cumsum(cf, axis=1), ((0, 0), (1, 0), (0, 0)))
    idx = jnp.arange(S)
    means = []
    for gi, w in enumerate(POOL_WINDOWS):
        seg = cs[..., gi * POOL_GROUP:(gi + 1) * POOL_GROUP]
        lo = jnp.clip(idx - w // 2, 0, S)
        hi = jnp.clip(idx + w // 2, 0, S)
        cnt = (hi - lo).astype(jnp.float32)[None, :, None]
        means.append((jnp.take(seg, hi, axis=1) - jnp.take(seg, lo, axis=1)) / cnt)
    pooled = jnp.concatenate(means, axis=-1) - cf
    B = c.shape[0]
    pg = pooled.astype(c.dtype).reshape(B, S, len(POOL_WINDOWS), POOL_GROUP)
    mixed = jnp.einsum('bsgc,gcd->bsgd', pg, w_pool).reshape(B, S, POOL_WIDTH)
    return mixed * pool_scale


def gated_conv_ffn(h, w_up, conv_w, conv_b, w_down):
    a = h @ w_up
    ch = a.shape[-1]
    a = lax.conv_general_dilated(
        a, conv_w[:, None, :].astype(a.dtype), window_strides=(1,),
        padding=((CONV_WIDTH // 2, CONV_WIDTH // 2),),
        dimension_numbers=('NWC', 'WIO', 'NWC'), feature_group_count=ch) + conv_b
    gate, val = jnp.split(a, 2, axis=-1)
    return (jax.nn.gelu(gate, approximate=False) * val) @ w_down


def setup_inputs(seed: int = 0) -> dict:
    key = jax.random.key(seed)
    ks = iter(jax.random.split(key, 32))

    def nrm(shape, scale):
        return jax.random.normal(next(ks), shape, jnp.float32) * scale

    return {
        "x": nrm((BATCH, SEQ, D_MODEL), 1.0),
        "p": nrm((DEPTH, BATCH, SEQ, PLE_DIM), 1.0),
        "ln0_g": 1.0 + nrm((D_MODEL,), 0.02),
        "ln0_b": nrm((D_MODEL,), 0.02),
        "w_in": nrm((DEPTH, D_MODEL, IN_WIDTH), D_MODEL ** -0.5),
        "w_attn_out": nrm((DEPTH, ATTN_OUT_WIDTH, D_MODEL), ATTN_OUT_WIDTH ** -0.5),
        "w_pool": nrm((DEPTH, len(POOL_WINDOWS), POOL_GROUP, POOL_GROUP), POOL_GROUP ** -0.5),
        "pool_scale": 1.0 + nrm((DEPTH, POOL_WIDTH), 0.02),
        "w_pool_out": nrm((DEPTH, POOL_WIDTH, D_MODEL), POOL_WIDTH ** -0.5),
        "w_o": nrm((DEPTH, D_MODEL, D_MODEL), DN_BETA * D_MODEL ** -0.5),
        "ln1_g": 1.0 + nrm((DEPTH, D_MODEL), 0.02),
        "ln1_b": nrm((DEPTH, D_MODEL), 0.02),
        "w_up": nrm((DEPTH, D_MODEL, 2 * D_FF), D_MODEL ** -0.5),
        "conv_w": nrm((DEPTH, CONV_WIDTH, 2 * D_FF), CONV_WIDTH ** -0.5),
        "conv_b": nrm((DEPTH, 2 * D_FF), 0.02),
        "w_down": nrm((DEPTH, D_FF, D_MODEL), DN_BETA * D_FF ** -0.5),
        "w_ple": nrm((DEPTH, PLE_DIM, D_MODEL), PLE_DIM ** -0.5),
        "w_ple_gate": nrm((DEPTH, D_MODEL, D_MODEL), D_MODEL ** -0.5),
        "ln2_g": 1.0 + nrm((DEPTH, D_MODEL), 0.02),
        "ln2_b": nrm((DEPTH, D_MODEL), 0.02),
    }


def reference(x, p, ln0_g, ln0_b, w_in, w_attn_out, w_pool, pool_scale, w_pool_out, w_o,
              ln1_g, ln1_b, w_up, conv_w, conv_b, w_down, w_ple, w_ple_gate, ln2_g, ln2_b):
    B, S, _ = x.shape
    cos, sin = rope_tables(S)
    splits = [ATTN_WIDTH, 2 * ATTN_WIDTH, 3 * ATTN_WIDTH, 3 * ATTN_WIDTH + POOL_WIDTH]
    h = layer_norm(x, ln0_g, ln0_b)
    for i in range(DEPTH):
        proj = h @ w_in[i]
        q, k, v, c, gl = jnp.split(proj, splits, axis=-1)
        q = apply_partial_rope(q.reshape(B, S, N_ATTN_HEADS, HEAD_DIM), cos, sin)
        k = apply_partial_rope(k.reshape(B, S, N_ATTN_HEADS, HEAD_DIM), cos, sin)
        v = v.reshape(B, S, N_ATTN_HEADS, HEAD_DIM)
        attn_b = dilated_attention_branch(q, k, v) @ w_attn_out[i]
        pool_b = multiscale_pool_branch(c, w_pool[i], pool_scale[i]) @ w_pool_out[i]
        g_a, g_b = jnp.split(jax.nn.sigmoid(gl), N_BRANCH, axis=-1)
        mixer = (g_a * attn_b + g_b * pool_b) @ w_o[i]
        h = layer_norm(DN_ALPHA * h + mixer, ln1_g[i], ln1_b[i])
        ffn = gated_conv_ffn(h, w_up[i], conv_w[i], conv_b[i], w_down[i])
        ple = (p[i] @ w_ple[i]) * jax.nn.sigmoid(h @ w_ple_gate[i])
        h = layer_norm(DN_ALPHA * h + ffn + ple, ln2_g[i], ln2_b[i])
    return h
```

```python
import math
from contextlib import ExitStack

import numpy as np
import concourse.bass as bass
import concourse.mybir as mybir
from concourse.bass_utils import run_bass_kernel_spmd

F32 = mybir.dt.float32
BF16 = mybir.dt.bfloat16
AF = mybir.ActivationFunctionType
ALU = mybir.AluOpType

S = 4096
D = 1024
KC = 8
TB = 512
NB = S // TB
L = 2
IN_W = 4864
DFF = 2816
NFF = 22
PLE = 256
ALPHA = (2.0 * L) ** 0.25
EPS = 1e-5
DIL = (1, 4, 16)
ROPE_THETA = 500000.0
COMPUTE = ("pe", "act", "dve", "pool")


class _Rec:
    def __getattr__(self, name):
        def f(*a, **k):
            return (name, a, k)
        return f


class Sched:
    NDMA = 24

    def __init__(self, nc, es):
        self.nc = nc
        self.engs = ("pe", "act", "dve", "pool", "sp")
        self.ops = {e: [] for e in self.engs}
        self.cnt = {e: 0 for e in COMPUTE}
        self.sem = {e: es.enter_context(nc.semaphore("c_" + e)) for e in COMPUTE}
        self.nd = {"sp": self.NDMA, "pool": 2, "act": 12}
        self.dsem = {q: [es.enter_context(nc.semaphore("d_%s%d" % (q, i))) for i in range(self.nd[q])]
                     for q in ("sp", "pool", "act")}
        self.ndma = {"sp": 0, "pool": 0, "act": 0}
        self.waited = {e: {} for e in self.engs}
        self.lastw = {}
        self.readers = {}
        self.pending = {e: [] for e in self.engs}
        self.all_dma_tokens = []

    def _deps(self, eng, reads, writes):
        toks = []
        for k in reads:
            t = self.lastw.get(k)
            if t is not None:
                toks.append(t)
        for k in writes:
            t = self.lastw.get(k)
            if t is not None:
                toks.append(t)
            toks.extend(self.readers.get(k, ()))
        toks.extend(self.pending[eng])
        self.pending[eng] = []
        need = {}
        for (sem, val, src) in toks:
            if src == "pe" and eng == "pe":
                continue
            key = id(sem)
            if self.waited[eng].get(key, 0) >= val:
                continue
            if key not in need or need[key][1] < val:
                need[key] = (sem, val)
        for key, (sem, val) in need.items():
            self.waited[eng][key] = val
        return list(need.values())

    def _record(self, tok, reads, writes):
        for k in reads:
            lst = self.readers.setdefault(k, [])
            lst[:] = [t for t in lst if not (t[0] is tok[0])]
            lst.append(tok)
        for k in writes:
            self.lastw[k] = tok
            self.readers[k] = []

    def op(self, eng, fn, reads=(), writes=()):
        waits = self._deps(eng, reads, writes)
        self.cnt[eng] += 1
        tok = (self.sem[eng], self.cnt[eng], eng)
        self.ops[eng].append((waits, fn(_Rec()), self.sem[eng], 1))
        self._record(tok, reads, writes)

    def dma(self, q, out, in_, reads=(), writes=()):
        i = self.ndma[q]
        self.ndma[q] += 1
        sem = self.dsem[q][i % self.nd[q]]
        prev = 16 * (i // self.nd[q])
        val = prev + 16
        waits = self._deps(q, reads, writes)
        if prev > 0 and self.waited[q].get(id(sem), 0) < prev:
            waits.append((sem, prev))
            self.waited[q][id(sem)] = prev
        tok = (sem, val, "dma")
        self.ops[q].append((waits, ("dma_start", (), dict(out=out, in_=in_)), sem, 16))
        self._record(tok, reads, writes)
        self.all_dma_tokens.append(tok)
        return tok

    def barrier(self):
        toks = [(self.sem[e], self.cnt[e], "bar") for e in COMPUTE if self.cnt[e] > 0]
        latest = {}
        pool_sems = set(id(x) for x in self.dsem["pool"])
        for t in self.all_dma_tokens:
            if id(t[0]) in pool_sems:
                continue
            latest[id(t[0])] = t
        toks.extend((t[0], t[1], "bar") for t in latest.values())
        for e in self.engs:
            self.pending[e].extend(toks)
        self.lastw = {k: v for k, v in self.lastw.items() if isinstance(k, tuple) and k[0] == "wb"}
        self.readers = {}

    def final_wait(self, tokens):
        self.pending["sp"].extend(tokens)
        self.ops["sp"].append((self._deps("sp", (), ()), None, None, 0))

    def emit(self, block):
        def runner(e):
            def f(eng):
                for (waits, fn, sem, inc) in self.ops[e]:
                    for (s_, v_) in waits:
                        eng.wait_ge(s_, v_)
                    if fn is not None:
                        getattr(eng, fn[0])(*fn[1], **fn[2]).then_inc(sem, inc)
            return f
        block.tensor(runner("pe"))
        block.scalar(runner("act"))
        block.vector(runner("dve"))
        block.gpsimd(runner("pool"))
        block.sync(runner("sp"))


WEIGHTS = [
    ("w_in", 1024, IN_W), ("w_attn_out", 256, 1024), ("w_pool", 512, 128), ("w_pool_out", 512, 1024),
    ("w_o", 1024, 1024), ("w_up", 1024, 2 * DFF), ("w_down", DFF, 1024), ("w_ple", 256, 1024),
    ("w_ple_gate", 1024, 1024),
]


def build_program(n_layers=L, dump=False, stop=None):
    nc = bass.Bass("TRN2", target_bir_lowering=False)
    es = ExitStack()
    with es:
        def din(name, shape, dt=F32):
            return nc.dram_tensor(name, list(shape), dt, kind="ExternalInput").ap()

        xT = din("xT", [D, S])
        pT = din("pT", [L, PLE, S])
        wf = {n: din(n, [L * k * c // 1024, 1024]) for (n, k, c) in WEIGHTS}
        wb = {n: nc.dram_tensor(n + "_b", [L, k, c], BF16, kind="Internal").ap() for (n, k, c) in WEIGHTS}
        cosd = din("cosT", [128, S])
        sind = din("sinT", [128, S])
        rmat_d = din("rmat", [128, 128])
        ident_d = din("ident", [128, 128])
        mask_d = din("mask3", [128, 768])
        lnv_d = din("lnv", [128, 2 + 4 * L, KC])
        psc_d = din("pscale", [128, L, 4])
        cw_d = din("convw", [128, L, 4, 2 * NFF])
        icnt_d = din("icnt", [128, 2, 4, 8])
        yT = nc.dram_tensor("yT", [D, S], F32, kind="ExternalOutput").ap()
        hA = nc.dram_tensor("hA", [D, S], F32, kind="Internal").ap()
        hB = nc.dram_tensor("hB", [D, S], F32, kind="Internal").ap()
        hAb = nc.dram_tensor("hAb", [D, S], BF16, kind="Internal").ap()
        hBb = nc.dram_tensor("hBb", [D, S], BF16, kind="Internal").ap()
        dbg = {}
        if dump:
            for nm in ("d_h0", "d_h1", "d_h2"):
                dbg[nm] = nc.dram_tensor(nm, [D, S], F32, kind="ExternalOutput").ap()
            dbg["d_o"] = nc.dram_tensor("d_o", [128, 2 * S], F32, kind="ExternalOutput").ap()

        def fm(ap):
            return ap.rearrange("(k p) t -> p k t", p=128)

        sc = Sched(nc, es)
        ucount = [0]

        def uname(n):
            ucount[0] += 1
            return "%s_u%d" % (n, ucount[0])

        def sb(name, shape, dt=F32):
            return es.enter_context(nc.sbuf_tensor(name, list(shape), dt))

        ones_f = sb("ones_f", [128, 128])
        ones_b = sb("ones_b", [128, 64], BF16)
        rmat = sb("rmat_s", [128, 128])
        ident_f = sb("ident_f", [128, 128])
        ident_b = sb("ident_b", [128, 128], BF16)
        mask_f = sb("mask_f", [128, 768])
        mask_b = sb("mask_b", [128, 2, 384], BF16)
        lnv = sb("lnv_s", [128, 2 + 4 * L, KC])
        psc = sb("psc_s", [128, L, 4])
        cw = sb("cw_s", [128, L, 4, 2 * NFF])
        icnt = sb("icnt_s", [128, 2, 4, 8])
        banks = [es.enter_context(nc.psum_tensor("bank%d" % i, [128, 512], F32)) for i in range(7)]
        bankT = es.enter_context(nc.psum_tensor("bankT", [128, 1024], BF16))
        B8 = banks + [bankT.bitcast(F32)]

        sc.op("pool", lambda e: e.memset(ones_f[:], 1.0), writes=["ones_f"])
        sc.op("pool", lambda e: e.memset(ones_b[:], 1.0), writes=["ones_b"])
        early_casts = []
        late_casts = []
        for l_ in range(L):
            for (n, k, c) in WEIGHTS:
                rows = k * c // 1024
                dst = wb[n][l_].rearrange("k c -> (k c)").rearrange("(r e) -> r e", e=1024)
                r0 = 0
                while r0 < rows:
                    r1 = min(rows, r0 + 2048)
                    args = (dst[r0:r1, :], wf[n][l_ * rows + r0:l_ * rows + r1, :], ("wb", n, l_))
                    if l_ == 0 and n == "w_in":
                        sc.dma("pool", args[0], args[1], writes=[args[2]])
                    elif l_ == 0:
                        early_casts.append(args)
                    else:
                        late_casts.append(args)
                    r0 = r1

        def issue_late_cast():
            if late_casts:
                a_ = late_casts.pop(0)
                sc.dma("pool", a_[0], a_[1], writes=[a_[2]])

        def issue_early_cast(after):
            if early_casts:
                a_ = early_casts.pop(0)
                sc.dma("pool", a_[0], a_[1], reads=after, writes=[a_[2]])

        sc.dma("sp", rmat[:], rmat_d, writes=["rmat"])
        sc.dma("sp", ident_f[:], ident_d, writes=["ident_f"])
        sc.dma("sp", mask_f[:], mask_d, writes=["mask_f"])
        sc.dma("sp", lnv[:], lnv_d, writes=["lnv"])
        sc.dma("sp", psc[:], psc_d, writes=["psc"])
        sc.dma("sp", cw[:], cw_d, writes=["cw"])
        sc.dma("sp", icnt[:], icnt_d, writes=["icnt"])
        sc.op("dve", lambda e: e.tensor_copy(out=ident_b[:], in_=ident_f[:]), reads=["ident_f"], writes=["ident_b"])
        sc.op("dve", lambda e: e.tensor_copy(out=mask_b[:].rearrange("p a b -> p (a b)"), in_=mask_f[:]),
              reads=["mask_f"], writes=["mask_b"])

        def ln_acc(r, rkey, kc, tmp, eng="pool"):
            sq, acs, acq = tmp["sq"], tmp["acs"], tmp["acq"]
            j = kc % 2
            if kc == 0:
                sc.op("act", lambda e: e.activation(out=acq[:], in_=r[:, kc, :], func=AF.Square), reads=[(rkey, kc)], writes=["acq"])
                if eng == "pool":
                    sc.op("pool", lambda e: e.tensor_copy(out=acs[:], in_=r[:, kc, :]), reads=[(rkey, kc)], writes=["acs"])
                else:
                    sc.op("act", lambda e: e.activation(out=acs[:], in_=r[:, kc, :], func=AF.Copy), reads=[(rkey, kc)], writes=["acs"])
            else:
                sc.op("act", lambda e: e.activation(out=sq[:, j, :], in_=r[:, kc, :], func=AF.Square), reads=[(rkey, kc)], writes=[("sq", j)])
                sc.op(eng, lambda e: e.tensor_tensor(out=acs[:], in0=acs[:], in1=r[:, kc, :], op=ALU.add), reads=[(rkey, kc), "acs"], writes=["acs"])
                sc.op(eng, lambda e: e.tensor_tensor(out=acq[:], in0=acq[:], in1=sq[:, j, :], op=ALU.add), reads=[("sq", j), "acq"], writes=["acq"])

        def ln_stats(r, rkey, tmp):
            mean, var, acs, acq = tmp["mean"], tmp["var"], tmp["acs"], tmp["acq"]
            b_sum, b_sq = banks[5], banks[6]
            sc.op("pe", lambda e: e.matmul(b_sum[:], ones_f[:], acs[:], start=True, stop=True), reads=["acs", "ones_f"], writes=["bank5"])
            sc.op("pe", lambda e: e.matmul(b_sq[:], ones_f[:], acq[:], start=True, stop=True), reads=["acq", "ones_f"], writes=["bank6"])
            sc.op("act", lambda e: e.activation(out=mean[:], in_=b_sum[:], func=AF.Copy, scale=1.0 / D),
                  reads=["bank5"], writes=["mean"])
            sc.op("dve", lambda e: e.tensor_tensor(out=var[:], in0=mean[:], in1=mean[:], op=ALU.mult),
                  reads=["mean"], writes=["var"])
            sc.op("dve", lambda e: e.scalar_tensor_tensor(out=var[:], in0=b_sq[:], scalar=1.0 / D, in1=var[:],
                                                          op0=ALU.mult, op1=ALU.subtract),
                  reads=["bank6", "var"], writes=["var"])
            sc.op("dve", lambda e: e.tensor_scalar(out=var[:], in0=var[:], scalar1=EPS, scalar2=None, op0=ALU.add),
                  reads=["var"], writes=["var"])
            sc.op("act", lambda e: e.activation(out=var[:], in_=var[:], func=AF.Sqrt), reads=["var"], writes=["var"])
            sc.op("dve", lambda e: e.reciprocal(out=var[:], in_=var[:]), reads=["var"], writes=["var"])
            sc.op("dve", lambda e: e.scalar_tensor_tensor(out=mean[:], in0=mean[:], scalar=-1.0, in1=var[:],
                                                          op0=ALU.mult, op1=ALU.mult),
                  reads=["mean", "var"], writes=["mean"])

        def ln_apply_steps(r, rkey, gi, bi, outs, outs_b, tmp, on_done=None, cast_eng="pool"):
            mean, var, yb = tmp["mean"], tmp["var"], tmp.get("yb")

            def chunk(kc):
                kk = (rkey, kc)
                if cast_eng == "pool":
                    gsc, bsc = lnv[:, gi, kc:kc + 1], lnv[:, bi, kc:kc + 1]
                    sc.op("pool", lambda e: e.tensor_tensor(out=r[:, kc, :], in0=r[:, kc, :], in1=var[:], op=ALU.mult),
                          reads=[kk, "var"], writes=[kk])
                    sc.op("pool", lambda e: e.tensor_tensor(out=r[:, kc, :], in0=r[:, kc, :], in1=mean[:], op=ALU.add),
                          reads=[kk, "mean"], writes=[kk])
                    if outs_b:
                        sc.op("pool", lambda e: e.tensor_scalar(out=yb[:, kc, :], in0=r[:, kc, :], scalar1=gsc, scalar2=bsc,
                                                                op0=ALU.mult, op1=ALU.add),
                              reads=[kk, "lnv"], writes=[("yb", kc)])
                    sc.op("pool", lambda e: e.tensor_scalar(out=r[:, kc, :], in0=r[:, kc, :], scalar1=gsc, scalar2=bsc,
                                                            op0=ALU.mult, op1=ALU.add),
                          reads=[kk, "lnv"], writes=[kk])
                    return
                sc.op("dve", lambda e: e.tensor_tensor(out=r[:, kc, :], in0=r[:, kc, :], in1=var[:], op=ALU.mult),
                      reads=[kk, "var"], writes=[kk])
                sc.op("dve", lambda e: e.tensor_tensor(out=r[:, kc, :], in0=r[:, kc, :], in1=mean[:], op=ALU.add),
                      reads=[kk, "mean"], writes=[kk])
                sc.op("act", lambda e: e.activation(out=r[:, kc, :], in_=r[:, kc, :], func=AF.Identity,
                                                    bias=lnv[:, bi, kc:kc + 1], scale=lnv[:, gi, kc:kc + 1]),
                      reads=[kk, "lnv"], writes=[kk])
                if outs_b:
                    sc.op("act", lambda e: e.activation(out=yb[:, kc, :], in_=r[:, kc, :], func=AF.Copy), reads=[kk], writes=[("yb", kc)])

            def fin():
                toks = []
                allk = [(rkey, kc) for kc in range(KC)]
                for (dap, dkey) in outs:
                    toks.append(sc.dma("act", dap, r[:], reads=allk, writes=[dkey]))
                for (dap, dkey) in outs_b:
                    sc.dma("act", dap, yb[:], reads=[("yb", kc) for kc in range(KC)], writes=[dkey])
                if on_done is not None:
                    on_done(toks)

            return [(lambda kc=kc: chunk(kc)) for kc in range(KC)] + [fin]

        final_tokens = []

        for l in range(n_layers):
            g1, b1, g2, b2 = 2 + 4 * l, 3 + 4 * l, 4 + 4 * l, 5 + 4 * l
            lay = ExitStack()
            oT_all = lay.enter_context(nc.sbuf_tensor("oT_all%d" % l, [128, 2, S], BF16))
            with ExitStack() as ph:
                def psb(name, shape, dt=F32):
                    return ph.enter_context(nc.sbuf_tensor(uname(name), list(shape), dt))
                QT = [psb("QT%d" % g, [128, S], BF16) for g in range(3)]
                KT = [psb("KT%d" % g, [128, S], BF16) for g in range(3)]
                VT = [psb("VT%d" % g, [128, S], BF16) for g in range(3)]
                Vtok = [psb("Vtok%d" % g, [128, 32, 128], BF16) for g in range(3)]
                accU = psb("accU", [128, S])
                accZ = psb("accZ", [128, S])
                wqkv = psb("wqkv", [128, KC, 9, 128], BF16)
                hb = psb("hbA", [128, 2, KC, TB], BF16)
                cosb = psb("cosb", [128, TB])
                sinb = psb("sinb", [128, TB])
                qf = psb("qf", [128, 2, TB])
                t1 = psb("t1", [128, 1, TB])
                t2 = psb("t2", [128, 1, TB])
                Pm = psb("Pm", [128, 2, 2, 384], BF16)
                nblk = [0]
                pendC = []
                ln0_r = accU[:].rearrange("p (k t) -> p k t", k=KC)
                ln0_tmp = dict(sq=accZ[:, 0:2 * TB].rearrange("p (a t) -> p a t", a=2), mean=accZ[:, 2 * TB:3 * TB], var=accZ[:, 3 * TB:4 * TB],
                               acs=accZ[:, 4 * TB:5 * TB], acq=accZ[:, 5 * TB:6 * TB],
                               yb=Vtok[0][:].rearrange("p a b -> p (a b)").rearrange("p (k t) -> p k t", k=KC))

                def ln0_steps(b):
                    bs = slice(b * TB, (b + 1) * TB)
                    steps = [lambda: sc.dma("act", ln0_r, fm(xT)[:, :, bs], writes=[("r0", kc) for kc in range(KC)])]
                    steps += [(lambda kc=kc: ln_acc(ln0_r, "r0", kc, ln0_tmp, eng="dve")) for kc in range(KC)]
                    steps.append(lambda: ln_stats(ln0_r, "r0", ln0_tmp))
                    outs = [(fm(hA)[:, :, bs], ("hA", b))]
                    if dump:
                        outs.append((fm(dbg["d_h0"])[:, :, bs], ("d_h0", b)))
                    steps += ln_apply_steps(ln0_r, "r0", 0, 1, outs, [(fm(hAb)[:, :, bs], ("hAb", b))], ln0_tmp,
                                            on_done=(lambda toks: final_tokens.append(toks[-1])) if dump else None, cast_eng="act")
                    return steps

                for pp in range(2):
                    wv = wb["w_in"][l].rearrange("(k p) n -> p k n", p=128)
                    for kind in range(3):
                        for g in range(3):
                            c0 = 768 * kind + 256 * g + 128 * pp
                            sc.dma("sp", wqkv[:, :, 3 * kind + g, :], wv[:, :, c0:c0 + 128],
                                   reads=[("wb", "w_in", l)], writes=[("wqkv", 3 * kind + g)])
                    n_evac = 0
                    n_rot = 0
                    pend = None

                    def rope_post(item):
                        (j, jr, dst, dkey, d) = item
                        b2k = banks[2 + jr]
                        b2key = "bank%d" % (2 + jr)
                        sc.op("pe", lambda e: e.matmul(b2k[:], rmat[:], qf[:, j, :], start=True, stop=True),
                              reads=[("qf", j), "rmat"], writes=[b2key])
                        sc.op("dve", lambda e: e.tensor_tensor(out=t1[:, 0, :], in0=qf[:, j, :], in1=cosb[:], op=ALU.mult),
                              reads=[("qf", j), "cosb"], writes=["t1"])
                        sc.op("dve", lambda e: e.tensor_tensor(out=t2[:, 0, :], in0=b2k[:], in1=sinb[:], op=ALU.mult),
                              reads=[b2key, "sinb"], writes=["t2"])
                        s1 = t1[:, 0, :].rearrange("p (m r) -> p r m", r=d)
                        s2 = t2[:, 0, :].rearrange("p (m r) -> p r m", r=d)
                        sc.op("dve", lambda e: e.tensor_tensor(out=dst, in0=s1, in1=s2, op=ALU.add),
                              reads=["t1", "t2"], writes=[dkey])

                    for b in range(NB):
                        bs = slice(b * TB, (b + 1) * TB)
                        hs = nblk[0] % 2
                        nblk[0] += 1
                        fuse0 = (l == 0 and pp == 0)
                        ln0_pend = []
                        if b == 0 and pp == 0:
                            if fuse0:
                                for st_ in ln0_steps(0):
                                    st_()
                            sc.dma("act", hb[:, hs], fm(hAb)[:, :, bs], reads=[("hAb", 0)], writes=[("hb", hs)])
                        if b + 1 < NB:
                            if fuse0:
                                ln0_pend = ln0_steps(b + 1)
                            else:
                                sc.dma("act", hb[:, 1 - hs], fm(hAb)[:, :, (b + 1) * TB:(b + 2) * TB], reads=[("hAb", b + 1)], writes=[("hb", 1 - hs)])
                        if pend is not None:
                            rope_post(pend)
                            pend = None
                        sc.dma("act", cosb[:], cosd[:, bs], writes=["cosb"])
                        sc.dma("act", sinb[:], sind[:, bs], writes=["sinb"])
                        for g in range(3):
                            d = DIL[g]
                            m0 = b * TB // d
                            mlen = TB // d
                            for kind in range(3):
                                j = n_evac % 2
                                n_evac += 1
                                bk = banks[j]
                                bkey = "bank%d" % j
                                for kc in range(KC):
                                    sc.op("pe", lambda e, kc=kc, bk=bk, wi=3 * kind + g: e.matmul(
                                        bk[:], wqkv[:, kc, wi, :], hb[:, hs, kc, :], start=(kc == 0), stop=(kc == KC - 1)),
                                        reads=[("wqkv", 3 * kind + g), ("hb", hs)], writes=[bkey])
                                for _q in range(3):
                                    if ln0_pend:
                                        ln0_pend.pop(0)()
                                dst_t = (QT, KT, VT)[kind][g]
                                dst = dst_t[:].rearrange("p (r m) -> p r m", r=d)[:, :, m0:m0 + mlen]
                                dkey = ("qkv", kind, g)
                                if kind == 2:
                                    src = bk[:].rearrange("p (m r) -> p r m", r=d)
                                    sc.op("act", lambda e, dst=dst, src=src: e.activation(out=dst, in_=src, func=AF.Copy),
                                          reads=[bkey], writes=[dkey])
                                else:
                                    jq = n_rot % 2
                                    n_rot += 1
                                    sc.op("act", lambda e, jq=jq, bk=bk: e.activation(out=qf[:, jq, :], in_=bk[:], func=AF.Copy),
                                          reads=[bkey], writes=[("qf", jq)])
                                    if pend is not None:
                                        rope_post(pend)
                                    pend = (jq, jq, dst, dkey, d)
                        while ln0_pend:
                            ln0_pend.pop(0)()
                        if fuse0 and b + 1 < NB:
                            sc.dma("act", hb[:, 1 - hs], fm(hAb)[:, :, (b + 1) * TB:(b + 2) * TB], reads=[("hAb", b + 1)], writes=[("hb", 1 - hs)])
                        if pendC:
                            pendC.pop(0)()
                        if l == 0:
                            issue_early_cast([("hb", hs)])
                    if pend is not None:
                        rope_post(pend)
                        pend = None
                    if l == 0 and pp == 0:
                        sc.barrier()
                    if pp == 0:
                        sc.dma("act", hb[:, nblk[0] % 2], fm(hAb)[:, :, 0:TB], reads=[("hAb", 0)], writes=[("hb", nblk[0] % 2)])
                    for g in (range(3) if stop != "A" else ()):
                        for c4 in range(8):
                            for i in range(4):
                                c = 4 * c4 + i
                                sc.op("pe", lambda e, g=g, c=c, i=i: e.transpose(
                                    out=bankT[:, 128 * i:128 * i + 128], in_=VT[g][:, 128 * c:128 * c + 128], identity=ident_b[:]),
                                    reads=[("qkv", 2, g), "ident_b"], writes=["bank7"])
                            sc.op("act", lambda e, g=g, c4=c4: e.activation(
                                out=Vtok[g][:, 4 * c4:4 * c4 + 4, :].rearrange("p a b -> p (a b)"), in_=bankT[:, 0:512], func=AF.Copy),
                                reads=["bank7"], writes=[("vtok", g)])
                    units = []
                    for g in (range(3) if stop not in ("A", "A2") else ()):
                        d = DIL[g]
                        cps = (S // d) // 128
                        for qb in range(32):
                            units.append((g, d, cps, qb))

                    def stage1(ui):
                        (g, d, cps, qb) = units[ui]
                        jj = qb % cps
                        tiles = [t for t in range(3) if 0 <= jj + t - 1 < cps]
                        lo, hi = 128 * tiles[0], 128 * tiles[-1] + 128
                        u = ui % 2
                        for hh in range(2):
                            bk = banks[2 * u + hh]
                            bkey = "bank%d" % (2 * u + hh)
                            ps_ = slice(64 * hh, 64 * hh + 64)
                            for t in tiles:
                                kcn = qb + t - 1
                                sc.op("pe", lambda e, bk=bk, t=t, kcn=kcn, ps_=ps_: e.matmul(
                                    bk[:, 128 * t:128 * t + 128], KT[g][ps_, 128 * kcn:128 * kcn + 128],
                                    QT[g][ps_, 128 * qb:128 * qb + 128], start=True, stop=True),
                                    reads=[("qkv", 0, g), ("qkv", 1, g)], writes=[bkey])
                            sc.op("act", lambda e, bk=bk, hh=hh: e.activation(
                                out=Pm[:, u, hh, lo:hi], in_=bk[:, lo:hi], func=AF.Exp, scale=0.125),
                                reads=[bkey], writes=[("Pm", u)])
                        sc.op("dve", lambda e: e.tensor_tensor(
                            out=Pm[:, u, :, lo:hi], in0=Pm[:, u, :, lo:hi], in1=mask_b[:, :, lo:hi], op=ALU.mult),
                            reads=[("Pm", u), "mask_b"], writes=[("Pm", u)])

                    def stage2(ui):
                        (g, d, cps, qb) = units[ui]
                        jj = qb % cps
                        rr = qb // cps
                        tiles = [t for t in range(3) if 0 <= jj + t - 1 < cps]
                        u = ui % 2
                        bu = banks[4 + u]
                        bukey = "bank%d" % (4 + u)
                        for zz in range(2):
                            for hh in range(2):
                                ps_ = slice(64 * hh, 64 * hh + 64)
                                for ti, t in enumerate(tiles):
                                    kcn = qb + t - 1
                                    lh = Vtok[g][:, kcn, 64 * hh:64 * hh + 64] if zz == 0 else ones_b[:]
                                    sc.op("pe", lambda e, zz=zz, ps_=ps_, lh=lh, hh=hh, t=t, ti=ti: e.matmul(
                                        bu[ps_, 128 * zz:128 * zz + 128], lh, Pm[:, u, hh, 128 * t:128 * t + 128],
                                        start=(ti == 0), stop=(ti == len(tiles) - 1)),
                                        reads=[("Pm", u), ("vtok", g), "ones_b"], writes=[bukey])
                        t0 = rr + 128 * jj * d
                        nat = slice(t0, t0 + 127 * d + 1, d)
                        if g == 0:
                            sc.op("dve", lambda e: e.tensor_copy(out=accU[:, nat], in_=bu[:, 0:128]), reads=[bukey], writes=["accU"])
                            sc.op("dve", lambda e: e.tensor_copy(out=accZ[:, nat], in_=bu[:, 128:256]), reads=[bukey], writes=["accZ"])
                        else:
                            sc.op("dve", lambda e: e.tensor_tensor(out=accU[:, nat], in0=bu[:, 0:128], in1=accU[:, nat], op=ALU.add),
                                  reads=[bukey, "accU"], writes=["accU"])
                            sc.op("dve", lambda e: e.tensor_tensor(out=accZ[:, nat], in0=bu[:, 128:256], in1=accZ[:, nat], op=ALU.add),
                                  reads=[bukey, "accZ"], writes=["accZ"])

                    for ui in range(len(units) + 1):
                        if ui < len(units):
                            stage1(ui)
                        if ui >= 1:
                            stage2(ui - 1)
                    def phase_c(b, pp=pp):
                        bs = slice(b * TB, (b + 1) * TB)
                        sc.op("dve", lambda e: e.reciprocal(out=accZ[:, bs], in_=accZ[:, bs]), reads=["accZ"], writes=["accZ"])
                        sc.op("dve", lambda e: e.tensor_tensor(out=accU[:, bs], in0=accU[:, bs], in1=accZ[:, bs], op=ALU.mult),
                              reads=["accU", "accZ"], writes=["accU"])
                        sc.op("act", lambda e: e.activation(out=oT_all[:, pp, bs], in_=accU[:, bs], func=AF.Copy),
                              reads=["accU"], writes=[("oT", l)])
                        if dump and l == 0:
                            final_tokens.append(sc.dma("act", dbg["d_o"][:, pp * S + b * TB: pp * S + (b + 1) * TB], accU[:, bs],
                                                       reads=["accU"], writes=[("d_o", pp, b)]))
                    if pp == 0:
                        pendC.extend((lambda b=b, f=phase_c: f(b)) for b in range(NB))
                    else:
                        for b in range(NB):
                            phase_c(b)
                sc.barrier()

            while early_casts:
                issue_early_cast([])
            with ExitStack() as ph:
                def psb(name, shape, dt=F32):
                    return ph.enter_context(nc.sbuf_tensor(uname(name), list(shape), dt))
                hf = psb("hfD", [128, KC, TB])
                hb2 = psb("hbD", [128, 2, KC, TB], BF16)
                hh2 = psb("hbhD", [128, 2, KC, 16], BF16)
                wr = psb("wrD", [128, 4, KC, 512], BF16)
                wao = psb("wao", [128, 2, 1024], BF16)
                wpo = psb("wpo", [128, 4, 1024], BF16)
                wpl = psb("wpl", [128, 4, 128], BF16)
                cT = psb("cT", [128, 4, 528])
                edg = psb("edg", [128, 8])
                sA = psb("sA", [128, 528])
                sB = psb("sB", [128, 528])
                pg = psb("pg", [128, 4, TB], BF16)
                mixedT = psb("mixedT", [128, 4, TB], BF16)
                sga = psb("sga", [128, 2, TB])
                sgb = psb("sgb", [128, 2, TB])
                mT = psb("mT", [128, KC, TB], BF16)
                r2 = [psb("rDa", [128, KC, TB]), psb("rDb", [128, KC, TB])]
                tmp = dict(sq=psb("sqD", [128, 2, TB]), mean=psb("meanD", [128, TB]), var=psb("varD", [128, TB]),
                           yb=psb("ybD", [128, KC, TB], BF16), acs=psb("acsD", [128, TB]), acq=psb("acqD", [128, TB]))
                sc.dma("sp", wao[:], wb["w_attn_out"][l].rearrange("(k p) n -> p k n", p=128), reads=[("wb", "w_attn_out", l)], writes=["wao"])
                sc.dma("sp", wpo[:], wb["w_pool_out"][l].rearrange("(k p) n -> p k n", p=128), reads=[("wb", "w_pool_out", l)], writes=["wpo"])
                sc.dma("sp", wpl[:], wb["w_pool"][l].rearrange("(k p) n -> p k n", p=128), reads=[("wb", "w_pool", l)], writes=["wpl"])
                wv = wb["w_in"][l].rearrange("(k p) n -> p k n", p=128)
                wov = wb["w_o"][l].rearrange("(k p) n -> p k n", p=128)
                nring = [0]

                def wload(wname, c0, width=512):
                    s_ = nring[0] % 4
                    nring[0] += 1
                    view = wb[wname][l].rearrange("(k p) n -> p k n", p=128)
                    sc.dma("sp", wr[:, s_, :, 0:width], view[:, :, c0:c0 + width], reads=[("wb", wname, l)], writes=[("wr", s_)])
                    return s_


                def load_block(b):
                    sl = b % 2
                    sc.dma("act", hb2[:, sl], fm(hAb)[:, :, b * TB:(b + 1) * TB], reads=[("hAb", b)], writes=[("hb", sl)])
                    if b > 0:
                        sc.dma("act", hh2[:, sl, :, 0:8], fm(hAb)[:, :, b * TB - 8:b * TB], reads=[("hAb", b - 1)], writes=[("hh0", sl)])
                    else:
                        sc.op("pool", lambda e: e.memset(hh2[:, sl, :, 0:8], 0.0), writes=[("hh0", sl)])
                    if b < NB - 1:
                        sc.dma("act", hh2[:, sl, :, 8:16], fm(hAb)[:, :, (b + 1) * TB:(b + 1) * TB + 8], reads=[("hAb", b + 1)], writes=[("hh1", sl)])
                    else:
                        sc.op("pool", lambda e: e.memset(hh2[:, sl, :, 8:16], 0.0), writes=[("hh1", sl)])

                runD = stop not in ("A", "A2", "B", "C")
                pend_steps = []
                if runD:
                    load_block(0)
                for b in (range(NB) if runD else ()):
                    bs = slice(b * TB, (b + 1) * TB)
                    sl = b % 2
                    if b + 1 < NB:
                        load_block(b + 1)
                    issue_late_cast()
                    sc.dma("act", hf[:], fm(hA)[:, :, bs], reads=[("hA", b)], writes=["hf"])
                    hb = hb2[:, sl]
                    hbh = hh2[:, sl]
                    r = r2[sl]
                    rk = "r%d" % sl
                    hbk = ("hb", sl)
                    hhk = [("hh0", sl), ("hh1", sl)]
                    s_c = wload("w_in", 2304)
                    for gi in range(4):
                        bk, bkey = B8[gi], "bank%d" % gi
                        bh, bhkey = B8[4 + gi], "bank%d" % (4 + gi)
                        for kc in range(KC):
                            sc.op("pe", lambda e, kc=kc: e.matmul(bk[:], wr[:, s_c, kc, 128 * gi:128 * gi + 128], hb[:, kc, :],
                                                                  start=(kc == 0), stop=(kc == KC - 1)),
                                  reads=[("wr", s_c), hbk], writes=[bkey])
                        for kc in range(KC):
                            sc.op("pe", lambda e, kc=kc: e.matmul(bh[:, 0:16], wr[:, s_c, kc, 128 * gi:128 * gi + 128], hbh[:, kc, :],
                                                                  start=(kc == 0), stop=(kc == KC - 1)),
                                  reads=[("wr", s_c)] + hhk, writes=[bhkey])
                        ck = ("cT", gi)
                        sc.op("act", lambda e: e.activation(out=cT[:, gi, 8:520], in_=bk[:], func=AF.Copy), reads=[bkey], writes=[ck])
                        sc.op("act", lambda e: e.activation(out=cT[:, gi, 0:8], in_=bh[:, 0:8], func=AF.Copy), reads=[bhkey], writes=[ck])
                        sc.op("act", lambda e: e.activation(out=cT[:, gi, 520:528], in_=bh[:, 8:16], func=AF.Copy), reads=[bhkey], writes=[ck])
                    for gi in range(4):
                        w = 2 << gi
                        ck = ("cT", gi)
                        src, skey = cT[:, gi, :], ck
                        dsts = [(sA, "sA"), (sB, "sB")]
                        step = 1
                        n = 528
                        di = 0
                        while step < w:
                            n2 = n - step
                            dt_, dk_ = dsts[di % 2]
                            di += 1
                            sc.op("dve", lambda e: e.tensor_tensor(out=dt_[:, 0:n2], in0=src[:, 0:n2], in1=src[:, step:step + n2], op=ALU.add),
                                  reads=[skey], writes=[dk_])
                            src, skey, n, step = dt_[:, :], dk_, n2, step * 2
                        st = 8 - w // 2
                        ot_, ok_ = dsts[di % 2]
                        sc.op("dve", lambda e: e.scalar_tensor_tensor(out=ot_[:, 0:TB], in0=src[:, st:st + TB], scalar=1.0 / w,
                                                                      in1=cT[:, gi, 8:520], op0=ALU.mult, op1=ALU.subtract),
                              reads=[skey, ck], writes=[ok_])
                        if b == 0 or b == NB - 1:
                            e0 = 0 if b == 0 else TB - 8
                            side = 0 if b == 0 else 1
                            sc.op("pool", lambda e: e.tensor_tensor(out=edg[:, 0:8], in0=src[:, st + e0:st + e0 + 8], in1=icnt[:, side, gi, :], op=ALU.mult),
                                  reads=[skey, "icnt"], writes=["edg"])
                            sc.op("pool", lambda e: e.tensor_tensor(out=ot_[:, e0:e0 + 8], in0=edg[:, 0:8], in1=cT[:, gi, 8 + e0:16 + e0], op=ALU.subtract),
                                  reads=["edg", ck, ok_], writes=[ok_])
                        sc.op("act", lambda e: e.activation(out=pg[:, gi, :], in_=ot_[:, 0:TB], func=AF.Copy), reads=[ok_], writes=[("pg", gi)])
                    gw = {}

                    def stage_a(oc):
                        if oc % 4 == 0:
                            gw["ga"] = wload("w_in", 2816 + 512 * (oc // 4))
                            gw["gb"] = wload("w_in", 3840 + 512 * (oc // 4))
                        s_ga, s_gb = gw["ga"], gw["gb"]
                        o4 = (oc % 4) * 128
                        ocs = slice(128 * oc, 128 * oc + 128)
                        ids = (0, 1, 2, 3) if oc % 2 == 0 else (4, 5, 6, 7)
                        Y0, Y1, Y2 = (B8[q] for q in ids[:3])
                        y0, y1, y2 = ("bank%d" % q for q in ids[:3])
                        for kc in range(KC):
                            sc.op("pe", lambda e, kc=kc: e.matmul(Y0[:], wr[:, s_ga, kc, o4:o4 + 128], hb[:, kc, :],
                                                                  start=(kc == 0), stop=(kc == KC - 1)),
                                  reads=[("wr", s_ga), hbk], writes=[y0])
                        for kc in range(KC):
                            sc.op("pe", lambda e, kc=kc: e.matmul(Y1[:], wr[:, s_gb, kc, o4:o4 + 128], hb[:, kc, :],
                                                                  start=(kc == 0), stop=(kc == KC - 1)),
                                  reads=[("wr", s_gb), hbk], writes=[y1])
                        for k2 in range(2):
                            sc.op("pe", lambda e, k2=k2: e.matmul(Y2[:], wao[:, k2, ocs], oT_all[:, k2, bs], start=(k2 == 0), stop=(k2 == 1)),
                                  reads=["wao", ("oT", l)], writes=[y2])
                        st_ = oc % 2
                        sc.op("act", lambda e: e.activation(out=sga[:, st_, :], in_=Y0[:], func=AF.Sigmoid), reads=[y0], writes=[("sga", st_)])
                        sc.op("act", lambda e: e.activation(out=sgb[:, st_, :], in_=Y1[:], func=AF.Sigmoid), reads=[y1], writes=[("sgb", st_)])
                        sc.op("dve", lambda e: e.tensor_tensor(out=sga[:, st_, :], in0=Y2[:], in1=sga[:, st_, :], op=ALU.mult),
                              reads=[y2, ("sga", st_)], writes=[("sga", st_)])

                    def stage_b(oc):
                        ocs = slice(128 * oc, 128 * oc + 128)
                        st_ = oc % 2
                        Y3, y3 = (B8[3], "bank3") if st_ == 0 else (B8[7], "bank7")
                        for gi in range(4):
                            sc.op("pe", lambda e, gi=gi: e.matmul(Y3[:], wpo[:, gi, ocs], mixedT[:, gi, :], start=(gi == 0), stop=(gi == 3)),
                                  reads=["wpo", ("mixedT", gi)], writes=[y3])
                        sc.op("dve", lambda e: e.tensor_tensor(out=sgb[:, st_, :], in0=Y3[:], in1=sgb[:, st_, :], op=ALU.mult),
                              reads=[y3, ("sgb", st_)], writes=[("sgb", st_)])
                        sc.op("dve", lambda e: e.tensor_tensor(out=mT[:, oc, :], in0=sga[:, st_, :], in1=sgb[:, st_, :], op=ALU.add),
                              reads=[("sga", st_), ("sgb", st_)], writes=[("mT", oc)])

                    def mixing():
                        for gi in range(4):
                            bm, bmkey = (B8[3], "bank3") if gi % 2 == 0 else (B8[7], "bank7")
                            sc.op("pe", lambda e: e.matmul(bm[:], wpl[:, gi, :], pg[:, gi, :], start=True, stop=True),
                                  reads=["wpl", ("pg", gi)], writes=[bmkey])
                            sc.op("act", lambda e: e.activation(out=mixedT[:, gi, :], in_=bm[:], func=AF.Identity, scale=psc[:, l, gi:gi + 1]),
                                  reads=[bmkey, "psc"], writes=[("mixedT", gi)])

                    stage_a(0)
                    if pend_steps:
                        pend_steps.pop(0)()
                    stage_a(1)
                    mixing()
                    for oc in range(KC):
                        stage_b(oc)
                        if oc + 2 < KC:
                            stage_a(oc + 2)
                        if pend_steps:
                            pend_steps.pop(0)()
                    while pend_steps:
                        pend_steps.pop(0)()
                    for oc in range(KC):
                        if oc % 4 == 0:
                            s_o = wload("w_o", 512 * (oc // 4))
                        o4 = (oc % 4) * 128
                        bk, bkey = banks[oc % 2], "bank%d" % (oc % 2)
                        for kc in range(KC):
                            sc.op("pe", lambda e, kc=kc, o4=o4, s_o=s_o, bk=bk: e.matmul(bk[:], wr[:, s_o, kc, o4:o4 + 128], mT[:, kc, :],
                                                                                    start=(kc == 0), stop=(kc == KC - 1)),
                                  reads=[("wr", s_o), ("mT", kc)], writes=[bkey])
                        sc.op("dve", lambda e, oc=oc, bk=bk: e.scalar_tensor_tensor(out=r[:, oc, :], in0=hf[:, oc, :], scalar=ALPHA, in1=bk[:],
                                                                                   op0=ALU.mult, op1=ALU.add),
                              reads=["hf", bkey], writes=[(rk, oc)])
                        ln_acc(r, rk, oc, tmp)

                    outs = [(fm(hB)[:, :, bs], ("hB", b))]
                    if dump and l == 0:
                        outs.append((fm(dbg["d_h1"])[:, :, bs], ("d_h1", b)))
                    pend_steps = [lambda r=r, rk=rk: ln_stats(r, rk, tmp)]
                    pend_steps += ln_apply_steps(r, rk, g1, b1, outs, [(fm(hBb)[:, :, bs], ("hBb", b))], tmp,
                                                on_done=(lambda toks: final_tokens.append(toks[-1])) if (dump and l == 0) else None)
                while pend_steps:
                    pend_steps.pop(0)()
                sc.barrier()

            with ExitStack() as ph:
                def psb(name, shape, dt=F32):
                    return ph.enter_context(nc.sbuf_tensor(uname(name), list(shape), dt))
                hf = psb("hfE", [128, KC, TB])
                hb2 = psb("hbE", [128, 2, KC, TB], BF16)
                hh2 = psb("hbhE", [128, 2, KC, 16], BF16)
                pf = psb("pfE", [128, 2, TB])
                pb = psb("pbE", [128, 2, TB], BF16)
                wr = psb("wrE", [128, 6, KC, 512], BF16)
                wple = psb("wple", [128, 2, 1024], BF16)
                aext = psb("aext", [128, 3, 514])
                yv = psb("yv", [128, 2, TB])
                yg = psb("yg", [128, 2, TB])
                gT = psb("gT", [128, NFF, TB], BF16)
                sg = psb("sgE", [128, 2, TB])
                r = psb("rE", [128, KC, TB])
                tmp = dict(sq=psb("sqE", [128, 2, TB]), mean=psb("meanE", [128, TB]), var=psb("varE", [128, TB]),
                           yb=psb("ybE", [128, KC, TB], BF16), acs=psb("acsE", [128, TB]), acq=psb("acqE", [128, TB]))
                sc.dma("sp", wple[:], wb["w_ple"][l].rearrange("(k p) n -> p k n", p=128), reads=[("wb", "w_ple", l)], writes=["wple"])
                wuv = wb["w_up"][l].rearrange("(k p) n -> p k n", p=128)
                wdv = wb["w_down"][l].rearrange("(k p) n -> p k n", p=128)
                wgv = wb["w_ple_gate"][l].rearrange("(k p) n -> p k n", p=128)
                nring = [0]

                def wload(wname, c0, width=512):
                    s_ = nring[0] % 6
                    nring[0] += 1
                    view = wb[wname][l].rearrange("(k p) n -> p k n", p=128)
                    sc.dma("sp", wr[:, s_, :, 0:width], view[:, :, c0:c0 + width], reads=[("wb", wname, l)], writes=[("wr", s_)])
                    return s_

                last = (l == n_layers - 1)

                def load_block(b):
                    sl = b % 2
                    sc.dma("act", hb2[:, sl], fm(hBb)[:, :, b * TB:(b + 1) * TB], reads=[("hBb", b)], writes=[("hb", sl)])
                    if b > 0:
                        sc.dma("act", hh2[:, sl, :, 0:8], fm(hBb)[:, :, b * TB - 8:b * TB], reads=[("hBb", b - 1)], writes=[("hh0", sl)])
                    else:
                        sc.op("pool", lambda e: e.memset(hh2[:, sl, :, 0:8], 0.0), writes=[("hh0", sl)])
                    if b < NB - 1:
                        sc.dma("act", hh2[:, sl, :, 8:16], fm(hBb)[:, :, (b + 1) * TB:(b + 1) * TB + 8], reads=[("hBb", b + 1)], writes=[("hh1", sl)])
                    else:
                        sc.op("pool", lambda e: e.memset(hh2[:, sl, :, 8:16], 0.0), writes=[("hh1", sl)])

                runE = stop not in ("A", "A2", "B", "C", "D")
                pend_steps = []
                if runE:
                    load_block(0)
                for b in (range(NB) if runE else ()):
                    bs = slice(b * TB, (b + 1) * TB)
                    sl = b % 2
                    if b + 1 < NB:
                        load_block(b + 1)
                    issue_late_cast()
                    sc.dma("act", hf[:], fm(hB)[:, :, bs], reads=[("hB", b)], writes=["hf"])
                    hb = hb2[:, sl]
                    hbh = hh2[:, sl, :, 7:9]
                    hbk = ("hb", sl)
                    hhk = [("hh0", sl), ("hh1", sl)]
                    sc.dma("act", pf[:], pT[l].rearrange("(k p) t -> p k t", p=128)[:, :, bs], writes=["pf"])
                    sc.op("pool", lambda e: e.tensor_copy(out=pb[:], in_=pf[:]), reads=["pf"], writes=["pb"])
                    n_a = 0
                    deferred = None
                    for i in range(NFF):
                        if i % 4 == 0:
                            wid = 512 if i + 4 <= NFF else 128 * (NFF - i)
                            s_v = wload("w_up", DFF + 128 * i, wid)
                            s_g = wload("w_up", 128 * i, wid)
                        o4 = (i % 4) * 128
                        jp = i % 2
                        for which in (0, 1):
                            s_w = s_v if which == 0 else s_g
                            ch = (NFF + i) if which == 0 else i
                            jm = n_a % 3
                            jh = (3, 4, 7)[n_a % 3]
                            n_a += 1
                            bk, bkey = B8[jm], "bank%d" % jm
                            bh, bhkey = B8[jh], "bank%d" % jh
                            for kc in range(KC):
                                sc.op("pe", lambda e, kc=kc: e.matmul(bk[:], wr[:, s_w, kc, o4:o4 + 128], hb[:, kc, :],
                                                                      start=(kc == 0), stop=(kc == KC - 1)),
                                      reads=[("wr", s_w), hbk], writes=[bkey])
                            for kc in range(KC):
                                sc.op("pe", lambda e, kc=kc: e.matmul(bh[:, 0:2], wr[:, s_w, kc, o4:o4 + 128], hbh[:, kc, :],
                                                                      start=(kc == 0), stop=(kc == KC - 1)),
                                      reads=[("wr", s_w)] + hhk, writes=[bhkey])
                            yo, yk = (yv, ("yv", jp)) if which == 0 else (yg, ("yg", jp))
                            w0, w1, w2, bb = (cw[:, l, q, ch:ch + 1] for q in range(4))
                            ja = (n_a - 1) % 3
                            ak = ("aext", ja)
                            sc.op("act", lambda e: e.activation(out=aext[:, ja, 1:513], in_=bk[:], func=AF.Copy), reads=[bkey], writes=[ak])
                            sc.op("act", lambda e: e.activation(out=aext[:, ja, 0:514:513], in_=bh[:, 0:2], func=AF.Copy), reads=[bhkey], writes=[ak])
                            if which == 0:
                                sc.op("act", lambda e: e.activation(out=yo[:, jp, :], in_=bk[:], func=AF.Identity, bias=bb, scale=w1),
                                      reads=[bkey, "cw"], writes=[yk])
                            else:
                                sc.op("dve", lambda e: e.tensor_scalar(out=yo[:, jp, :], in0=aext[:, ja, 1:513], scalar1=w1, scalar2=bb,
                                                                       op0=ALU.mult, op1=ALU.add),
                                      reads=[ak, "cw"], writes=[yk])
                            sc.op("dve", lambda e: e.scalar_tensor_tensor(out=yo[:, jp, :], in0=aext[:, ja, 0:512], scalar=w0,
                                                                          in1=yo[:, jp, :], op0=ALU.mult, op1=ALU.add),
                                  reads=[ak, "cw", yk], writes=[yk])
                            sc.op("dve", lambda e: e.scalar_tensor_tensor(out=yo[:, jp, :], in0=aext[:, ja, 2:514], scalar=w2,
                                                                          in1=yo[:, jp, :], op0=ALU.mult, op1=ALU.add),
                                  reads=[ak, "cw", yk], writes=[yk])
                            if deferred is not None and which == 0:
                                deferred()
                                deferred = None
                        def fin(i=i, jp=jp):
                            sc.op("act", lambda e: e.activation(out=yg[:, jp, :], in_=yg[:, jp, :], func=AF.Gelu),
                                  reads=[("yg", jp)], writes=[("yg", jp)])
                            sc.op("pool", lambda e: e.tensor_tensor(out=gT[:, i, :], in0=yg[:, jp, :], in1=yv[:, jp, :], op=ALU.mult),
                                  reads=[("yg", jp), ("yv", jp)], writes=[("gT", i)])
                        deferred = fin
                        if i >= 1 and i % 2 == 1 and pend_steps:
                            pend_steps.pop(0)()
                    deferred()
                    deferred = None
                    while pend_steps:
                        pend_steps.pop(0)()
                    n_o = 0
                    for half in range(2):
                        s_wd = []
                        for piece in range(3):
                            cnt = min(8, NFF - 8 * piece)
                            s_ = nring[0] % 6
                            nring[0] += 1
                            sc.dma("sp", wr[:, s_, 0:cnt, :], wdv[:, 8 * piece:8 * piece + cnt, 512 * half:512 * half + 512],
                                   reads=[("wb", "w_down", l)], writes=[("wr", s_)])
                            s_wd.append(s_)
                        s_pg = wload("w_ple_gate", 512 * half)
                        for o_ in range(4):
                            oc = 4 * half + o_
                            o4 = 128 * o_
                            ocs = slice(128 * oc, 128 * oc + 128)
                            st_ = n_o % 2
                            n_o += 1
                            ids = (0, 1, 2) if st_ == 0 else (3, 4, 7)
                            X0, X1, X2 = (B8[q] for q in ids)
                            k0, k1, k2_ = ("bank%d" % q for q in ids)
                            for i in range(NFF):
                                sc.op("pe", lambda e, i=i, sw=s_wd[i // 8]: e.matmul(X0[:], wr[:, sw, i % 8, o4:o4 + 128], gT[:, i, :],
                                                                                 start=(i == 0), stop=(i == NFF - 1)),
                                      reads=[("wr", s_wd[i // 8]), ("gT", i)], writes=[k0])
                            for k2 in range(2):
                                sc.op("pe", lambda e, k2=k2: e.matmul(X1[:], wple[:, k2, ocs], pb[:, k2, :], start=(k2 == 0), stop=(k2 == 1)),
                                      reads=["wple", "pb"], writes=[k1])
                            for kc in range(KC):
                                sc.op("pe", lambda e, kc=kc: e.matmul(X2[:], wr[:, s_pg, kc, o4:o4 + 128], hb[:, kc, :],
                                                                      start=(kc == 0), stop=(kc == KC - 1)),
                                      reads=[("wr", s_pg), hbk], writes=[k2_])
                            sc.op("act", lambda e: e.activation(out=sg[:, st_, :], in_=X2[:], func=AF.Sigmoid), reads=[k2_], writes=[("sg", st_)])
                            sc.op("dve", lambda e: e.tensor_tensor(out=sg[:, st_, :], in0=X1[:], in1=sg[:, st_, :], op=ALU.mult),
                                  reads=[k1, ("sg", st_)], writes=[("sg", st_)])
                            sc.op("dve", lambda e: e.scalar_tensor_tensor(out=r[:, oc, :], in0=hf[:, oc, :], scalar=ALPHA, in1=X0[:],
                                                                          op0=ALU.mult, op1=ALU.add),
                                  reads=["hf", k0], writes=[("r", oc)])
                            sc.op("dve", lambda e: e.tensor_tensor(out=r[:, oc, :], in0=r[:, oc, :], in1=sg[:, st_, :], op=ALU.add),
                                  reads=[("r", oc), ("sg", st_)], writes=[("r", oc)])
                            ln_acc(r, "r", oc, tmp)

                    if last:
                        outs, outs_b = [(fm(yT)[:, :, bs], ("yT", b))], []
                    else:
                        outs, outs_b = [(fm(hA)[:, :, bs], ("hA", b))], [(fm(hAb)[:, :, bs], ("hAb", b))]
                    if dump and l == 0:
                        outs.append((fm(dbg["d_h2"])[:, :, bs], ("d_h2", b)))

                    def done(toks, last=last):
                        if last:
                            final_tokens.append(toks[0])
                        if dump and l == 0:
                            final_tokens.append(toks[-1])
                    pend_steps = [lambda: ln_stats(r, "r", tmp)]
                    pend_steps += ln_apply_steps(r, "r", g2, b2, outs, outs_b, tmp, on_done=done)
                while pend_steps:
                    pend_steps.pop(0)()
                sc.barrier()
            lay.close()
            while late_casts:
                issue_late_cast()

        sc.final_wait([(t[0], t[1], "fin") for t in final_tokens])
        with nc.Block() as block:
            sc.emit(block)
    return nc


def _consts():
    pos = np.arange(S, dtype=np.float32)
    inv_freq = (ROPE_THETA ** (-np.arange(0, 16, 2, dtype=np.float32) / 16.0)).astype(np.float32)
    ang = pos[None, :] * inv_freq[:, None]
    cos, sin = np.cos(ang).astype(np.float32), np.sin(ang).astype(np.float32)
    cosT = np.ones((128, S), np.float32)
    sinT = np.zeros((128, S), np.float32)
    rmat = np.zeros((128, 128), np.float32)
    for p in range(128):
        dd = p % 64
        if dd < 8:
            cosT[p] = cos[dd]
            sinT[p] = -sin[dd]
            rmat[p + 8, p] = 1.0
        elif dd < 16:
            cosT[p] = cos[dd - 8]
            sinT[p] = sin[dd - 8]
            rmat[p - 8, p] = 1.0
    ident = np.eye(128, dtype=np.float32)
    i = np.arange(128)[:, None]
    u = np.arange(128)[None, :]
    m1 = np.concatenate([(np.abs(128 * (t - 1) + i - u) <= 64).astype(np.float32) for t in range(3)], axis=1)
    mask3 = np.concatenate([m1, m1], axis=1)
    icnt = np.zeros((128, 2, 4, 8), np.float32)
    for gi in range(4):
        w = 2 << gi
        for c in range(8):
            for side, tok in ((0, c), (1, S - 8 + c)):
                lo, hi = max(tok - w // 2, 0), min(tok + w // 2, S)
                icnt[:, side, gi, c] = 1.0 / float(hi - lo)
    return dict(cosT=cosT, sinT=sinT, rmat=rmat, ident=ident, mask3=mask3, icnt=icnt)


def _vec(v):
    return np.ascontiguousarray(np.asarray(v, np.float32).reshape(-1, 128).T)


def make_in_maps(inputs, n_cores=8):
    x = np.asarray(inputs["x"], np.float32)
    p = np.asarray(inputs["p"], np.float32)
    shared = dict(_consts())
    for (n, k, c) in WEIGHTS:
        shared[n] = np.ascontiguousarray(np.asarray(inputs[n], np.float32).reshape(-1, 1024))
    lnv = np.zeros((128, 2 + 4 * L, KC), np.float32)
    lnv[:, 0] = _vec(inputs["ln0_g"])
    lnv[:, 1] = _vec(inputs["ln0_b"])
    for l in range(L):
        lnv[:, 2 + 4 * l] = _vec(inputs["ln1_g"][l])
        lnv[:, 3 + 4 * l] = _vec(inputs["ln1_b"][l])
        lnv[:, 4 + 4 * l] = _vec(inputs["ln2_g"][l])
        lnv[:, 5 + 4 * l] = _vec(inputs["ln2_b"][l])
    shared["lnv"] = lnv
    psc = np.zeros((128, L, 4), np.float32)
    cwv = np.zeros((128, L, 4, 2 * NFF), np.float32)
    for l in range(L):
        psc[:, l] = _vec(inputs["pool_scale"][l])
        for j in range(3):
            cwv[:, l, j] = _vec(inputs["conv_w"][l][j])
        cwv[:, l, 3] = _vec(inputs["conv_b"][l])
    shared["pscale"] = psc
    shared["convw"] = cwv
    maps = []
    for c in range(n_cores):
        m = dict(shared)
        m["xT"] = np.ascontiguousarray(x[c].T)
        m["pT"] = np.ascontiguousarray(np.transpose(p[:, c], (0, 2, 1)))
        maps.append(m)
    return maps


def kernel(**inputs):
    nc = build_program()
    maps = make_in_maps(inputs, 8)
    res = run_bass_kernel_spmd(nc, maps, core_ids=list(range(8)))
    out = np.stack([np.ascontiguousarray(np.asarray(r["yT"], np.float32).T) for r in res.results], axis=0)
    return out.astype(np.float32)
```

```python
import math
from contextlib import ExitStack

import numpy as np
import concourse.bass as bass
import concourse.mybir as mybir
from concourse.bass_utils import run_bass_kernel_spmd

F32 = mybir.dt.float32
BF16 = mybir.dt.bfloat16
AF = mybir.ActivationFunctionType
ALU = mybir.AluOpType

S = 4096
D = 1024
KC = 8
TB = 512
NB = S // TB
L = 2
IN_W = 4864
DFF = 2816
NFF = 22
PLE = 256
ALPHA = (2.0 * L) ** 0.25
EPS = 1e-5
DIL = (1, 4, 16)
ROPE_THETA = 500000.0
COMPUTE = ("pe", "act", "dve", "pool")


class _Rec:
    def __getattr__(self, name):
        def f(*a, **k):
            return (name, a, k)
        return f


class Sched:
    NDMA = 24

    def __init__(self, nc, es):
        self.nc = nc
        self.engs = ("pe", "act", "dve", "pool", "sp")
        self.ops = {e: [] for e in self.engs}
        self.cnt = {e: 0 for e in COMPUTE}
        self.sem = {e: es.enter_context(nc.semaphore("c_" + e)) for e in COMPUTE}
        self.nd = {"sp": self.NDMA, "pool": 2, "act": 12}
        self.dsem = {q: [es.enter_context(nc.semaphore("d_%s%d" % (q, i))) for i in range(self.nd[q])]
                     for q in ("sp", "pool", "act")}
        self.ndma = {"sp": 0, "pool": 0, "act": 0}
        self.waited = {e: {} for e in self.engs}
        self.lastw = {}
        self.readers = {}
        self.pending = {e: [] for e in self.engs}
        self.all_dma_tokens = []

    def _deps(self, eng, reads, writes):
        toks = []
        for k in reads:
            t = self.lastw.get(k)
            if t is not None:
                toks.append(t)
        for k in writes:
            t = self.lastw.get(k)
            if t is not None:
                toks.append(t)
            toks.extend(self.readers.get(k, ()))
        toks.extend(self.pending[eng])
        self.pending[eng] = []
        need = {}
        for (sem, val, src) in toks:
            if src == "pe" and eng == "pe":
                continue
            key = id(sem)
            if self.waited[eng].get(key, 0) >= val:
                continue
            if key not in need or need[key][1] < val:
                need[key] = (sem, val)
        for key, (sem, val) in need.items():
            self.waited[eng][key] = val
        return list(need.values())

    def _record(self, tok, reads, writes):
        for k in reads:
            lst = self.readers.setdefault(k, [])
            lst[:] = [t for t in lst if not (t[0] is tok[0])]
            lst.append(tok)
        for k in writes:
            self.lastw[k] = tok
            self.readers[k] = []

    def op(self, eng, fn, reads=(), writes=()):
        waits = self._deps(eng, reads, writes)
        self.cnt[eng] += 1
        tok = (self.sem[eng], self.cnt[eng], eng)
        self.ops[eng].append((waits, fn(_Rec()), self.sem[eng], 1))
        self._record(tok, reads, writes)

    def dma(self, q, out, in_, reads=(), writes=()):
        i = self.ndma[q]
        self.ndma[q] += 1
        sem = self.dsem[q][i % self.nd[q]]
        prev = 16 * (i // self.nd[q])
        val = prev + 16
        waits = self._deps(q, reads, writes)
        if prev > 0 and self.waited[q].get(id(sem), 0) < prev:
            waits.append((sem, prev))
            self.waited[q][id(sem)] = prev
        tok = (sem, val, "dma")
        self.ops[q].append((waits, ("dma_start", (), dict(out=out, in_=in_)), sem, 16))
        self._record(tok, reads, writes)
        self.all_dma_tokens.append(tok)
        return tok

    def barrier(self):
        toks = [(self.sem[e], self.cnt[e], "bar") for e in COMPUTE if self.cnt[e] > 0]
        latest = {}
        pool_sems = set(id(x) for x in self.dsem["pool"])
        for t in self.all_dma_tokens:
            if id(t[0]) in pool_sems:
                continue
            latest[id(t[0])] = t
        toks.extend((t[0], t[1], "bar") for t in latest.values())
        for e in self.engs:
            self.pending[e].extend(toks)
        self.lastw = {k: v for k, v in self.lastw.items() if isinstance(k, tuple) and k[0] == "wb"}
        self.readers = {}

    def final_wait(self, tokens):
        self.pending["sp"].extend(tokens)
        self.ops["sp"].append((self._deps("sp", (), ()), None, None, 0))

    def emit(self, block):
        def runner(e):
            def f(eng):
                for (waits, fn, sem, inc) in self.ops[e]:
                    for (s_, v_) in waits:
                        eng.wait_ge(s_, v_)
                    if fn is not None:
                        getattr(eng, fn[0])(*fn[1], **fn[2]).then_inc(sem, inc)
            return f
        block.tensor(runner("pe"))
        block.scalar(runner("act"))
        block.vector(runner("dve"))
        block.gpsimd(runner("pool"))
        block.sync(runner("sp"))


WEIGHTS = [
    ("w_in", 1024, IN_W), ("w_attn_out", 256, 1024), ("w_pool", 512, 128), ("w_pool_out", 512, 1024),
    ("w_o", 1024, 1024), ("w_up", 1024, 2 * DFF), ("w_down", DFF, 1024), ("w_ple", 256, 1024),
    ("w_ple_gate", 1024, 1024),
]


def build_program(n_layers=L, dump=False, stop=None):
    nc = bass.Bass("TRN2", target_bir_lowering=False)
    es = ExitStack()
    with es:
        def din(name, shape, dt=F32):
            return nc.dram_tensor(name, list(shape), dt, kind="ExternalInput").ap()

        xT = din("xT", [D, S])
        pT = din("pT", [L, PLE, S])
        wf = {n: din(n, [L * k * c // 1024, 1024]) for (n, k, c) in WEIGHTS}
        wb = {n: nc.dram_tensor(n + "_b", [L, k, c], BF16, kind="Internal").ap() for (n, k, c) in WEIGHTS}
        cosd = din("cosT", [128, S])
        sind = din("sinT", [128, S])
        rmat_d = din("rmat", [128, 128])
        ident_d = din("ident", [128, 128])
        mask_d = din("mask3", [128, 768])
        lnv_d = din("lnv", [128, 2 + 4 * L, KC])
        psc_d = din("pscale", [128, L, 4])
        cw_d = din("convw", [128, L, 4, 2 * NFF])
        icnt_d = din("icnt", [128, 2, 4, 8])
        yT = nc.dram_tensor("yT", [D, S], F32, kind="ExternalOutput").ap()
        hA = nc.dram_tensor("hA", [D, S], F32, kind="Internal").ap()
        hB = nc.dram_tensor("hB", [D, S], F32, kind="Internal").ap()
        hAb = nc.dram_tensor("hAb", [D, S], BF16, kind="Internal").ap()
        hBb = nc.dram_tensor("hBb", [D, S], BF16, kind="Internal").ap()
        dbg = {}
        if dump:
            for nm in ("d_h0", "d_h1", "d_h2"):
                dbg[nm] = nc.dram_tensor(nm, [D, S], F32, kind="ExternalOutput").ap()
            dbg["d_o"] = nc.dram_tensor("d_o", [128, 2 * S], F32, kind="ExternalOutput").ap()

        def fm(ap):
            return ap.rearrange("(k p) t -> p k t", p=128)

        sc = Sched(nc, es)
        ucount = [0]

        def uname(n):
            ucount[0] += 1
            return "%s_u%d" % (n, ucount[0])

        def sb(name, shape, dt=F32):
            return es.enter_context(nc.sbuf_tensor(name, list(shape), dt))

        ones_f = sb("ones_f", [128, 128])
        ones_b = sb("ones_b", [128, 64], BF16)
        rmat = sb("rmat_s", [128, 128])
        ident_f = sb("ident_f", [128, 128])
        ident_b = sb("ident_b", [128, 128], BF16)
        mask_f = sb("mask_f", [128, 768])
        mask_b = sb("mask_b", [128, 2, 384], BF16)
        lnv = sb("lnv_s", [128, 2 + 4 * L, KC])
        psc = sb("psc_s", [128, L, 4])
        cw = sb("cw_s", [128, L, 4, 2 * NFF])
        icnt = sb("icnt_s", [128, 2, 4, 8])
        banks = [es.enter_context(nc.psum_tensor("bank%d" % i, [128, 512], F32)) for i in range(7)]
        bankT = es.enter_context(nc.psum_tensor("bankT", [128, 1024], BF16))
        B8 = banks + [bankT.bitcast(F32)]

        sc.op("pool", lambda e: e.memset(ones_f[:], 1.0), writes=["ones_f"])
        sc.op("pool", lambda e: e.memset(ones_b[:], 1.0), writes=["ones_b"])
        early_casts = []
        late_casts = []
        for l_ in range(L):
            for (n, k, c) in WEIGHTS:
                rows = k * c // 1024
                dst = wb[n][l_].rearrange("k c -> (k c)").rearrange("(r e) -> r e", e=1024)
                r0 = 0
                while r0 < rows:
                    r1 = min(rows, r0 + 2048)
                    args = (dst[r0:r1, :], wf[n][l_ * rows + r0:l_ * rows + r1, :], ("wb", n, l_))
                    if l_ == 0 and n == "w_in":
                        sc.dma("pool", args[0], args[1], writes=[args[2]])
                    elif l_ == 0:
                        early_casts.append(args)
                    else:
                        late_casts.append(args)
                    r0 = r1

        def issue_late_cast():
            if late_casts:
                a_ = late_casts.pop(0)
                sc.dma("pool", a_[0], a_[1], writes=[a_[2]])

        def issue_early_cast(after):
            if early_casts:
                a_ = early_casts.pop(0)
                sc.dma("pool", a_[0], a_[1], reads=after, writes=[a_[2]])

        sc.dma("sp", rmat[:], rmat_d, writes=["rmat"])
        sc.dma("sp", ident_f[:], ident_d, writes=["ident_f"])
        sc.dma("sp", mask_f[:], mask_d, writes=["mask_f"])
        sc.dma("sp", lnv[:], lnv_d, writes=["lnv"])
        sc.dma("sp", psc[:], psc_d, writes=["psc"])
        sc.dma("sp", cw[:], cw_d, writes=["cw"])
        sc.dma("sp", icnt[:], icnt_d, writes=["icnt"])
        sc.op("dve", lambda e: e.tensor_copy(out=ident_b[:], in_=ident_f[:]), reads=["ident_f"], writes=["ident_b"])
        sc.op("dve", lambda e: e.tensor_copy(out=mask_b[:].rearrange("p a b -> p (a b)"), in_=mask_f[:]),
              reads=["mask_f"], writes=["mask_b"])

        def ln_acc(r, rkey, kc, tmp, eng="pool"):
            sq, acs, acq = tmp["sq"], tmp["acs"], tmp["acq"]
            j = kc % 2
            if kc == 0:
                sc.op("act", lambda e: e.activation(out=acq[:], in_=r[:, kc, :], func=AF.Square), reads=[(rkey, kc)], writes=["acq"])
                if eng == "pool":
                    sc.op("pool", lambda e: e.tensor_copy(out=acs[:], in_=r[:, kc, :]), reads=[(rkey, kc)], writes=["acs"])
                else:
                    sc.op("act", lambda e: e.activation(out=acs[:], in_=r[:, kc, :], func=AF.Copy), reads=[(rkey, kc)], writes=["acs"])
            else:
                sc.op("act", lambda e: e.activation(out=sq[:, j, :], in_=r[:, kc, :], func=AF.Square), reads=[(rkey, kc)], writes=[("sq", j)])
                sc.op(eng, lambda e: e.tensor_tensor(out=acs[:], in0=acs[:], in1=r[:, kc, :], op=ALU.add), reads=[(rkey, kc), "acs"], writes=["acs"])
                sc.op(eng, lambda e: e.tensor_tensor(out=acq[:], in0=acq[:], in1=sq[:, j, :], op=ALU.add), reads=[("sq", j), "acq"], writes=["acq"])

        def ln_stats(r, rkey, tmp):
            mean, var, acs, acq = tmp["mean"], tmp["var"], tmp["acs"], tmp["acq"]
            b_sum, b_sq = banks[5], banks[6]
            sc.op("pe", lambda e: e.matmul(b_sum[:], ones_f[:], acs[:], start=True, stop=True), reads=["acs", "ones_f"], writes=["bank5"])
            sc.op("pe", lambda e: e.matmul(b_sq[:], ones_f[:], acq[:], start=True, stop=True), reads=["acq", "ones_f"], writes=["bank6"])
            sc.op("act", lambda e: e.activation(out=mean[:], in_=b_sum[:], func=AF.Copy, scale=1.0 / D),
                  reads=["bank5"], writes=["mean"])
            sc.op("dve", lambda e: e.tensor_tensor(out=var[:], in0=mean[:], in1=mean[:], op=ALU.mult),
                  reads=["mean"], writes=["var"])
            sc.op("dve", lambda e: e.scalar_tensor_tensor(out=var[:], in0=b_sq[:], scalar=1.0 / D, in1=var[:],
                                                          op0=ALU.mult, op1=ALU.subtract),
                  reads=["bank6", "var"], writes=["var"])
            sc.op("dve", lambda e: e.tensor_scalar(out=var[:], in0=var[:], scalar1=EPS, scalar2=None, op0=ALU.add),
                  reads=["var"], writes=["var"])
            sc.op("act", lambda e: e.activation(out=var[:], in_=var[:], func=AF.Sqrt), reads=["var"], writes=["var"])
            sc.op("dve", lambda e: e.reciprocal(out=var[:], in_=var[:]), reads=["var"], writes=["var"])
            sc.op("dve", lambda e: e.scalar_tensor_tensor(out=mean[:], in0=mean[:], scalar=-1.0, in1=var[:],
                                                          op0=ALU.mult, op1=ALU.mult),
                  reads=["mean", "var"], writes=["mean"])

        def ln_apply_steps(r, rkey, gi, bi, outs, outs_b, tmp, on_done=None, cast_eng="pool"):
            mean, var, yb = tmp["mean"], tmp["var"], tmp.get("yb")

            def chunk(kc):
                kk = (rkey, kc)
                if cast_eng == "pool":
                    gsc, bsc = lnv[:, gi, kc:kc + 1], lnv[:, bi, kc:kc + 1]
                    sc.op("pool", lambda e: e.tensor_tensor(out=r[:, kc, :], in0=r[:, kc, :], in1=var[:], op=ALU.mult),
                          reads=[kk, "var"], writes=[kk])
                    sc.op("pool", lambda e: e.tensor_tensor(out=r[:, kc, :], in0=r[:, kc, :], in1=mean[:], op=ALU.add),
                          reads=[kk, "mean"], writes=[kk])
                    if outs_b:
                        sc.op("pool", lambda e: e.tensor_scalar(out=yb[:, kc, :], in0=r[:, kc, :], scalar1=gsc, scalar2=bsc,
                                                                op0=ALU.mult, op1=ALU.add),
                              reads=[kk, "lnv"], writes=[("yb", kc)])
                    sc.op("pool", lambda e: e.tensor_scalar(out=r[:, kc, :], in0=r[:, kc, :], scalar1=gsc, scalar2=bsc,
                                                            op0=ALU.mult, op1=ALU.add),
                          reads=[kk, "lnv"], writes=[kk])
                    return
                sc.op("dve", lambda e: e.tensor_tensor(out=r[:, kc, :], in0=r[:, kc, :], in1=var[:], op=ALU.mult),
                      reads=[kk, "var"], writes=[kk])
                sc.op("dve", lambda e: e.tensor_tensor(out=r[:, kc, :], in0=r[:, kc, :], in1=mean[:], op=ALU.add),
                      reads=[kk, "mean"], writes=[kk])
                sc.op("act", lambda e: e.activation(out=r[:, kc, :], in_=r[:, kc, :], func=AF.Identity,
                                                    bias=lnv[:, bi, kc:kc + 1], scale=lnv[:, gi, kc:kc + 1]),
                      reads=[kk, "lnv"], writes=[kk])
                if outs_b:
                    sc.op("act", lambda e: e.activation(out=yb[:, kc, :], in_=r[:, kc, :], func=AF.Copy), reads=[kk], writes=[("yb", kc)])

            def fin():
                toks = []
                allk = [(rkey, kc) for kc in range(KC)]
                for (dap, dkey) in outs:
                    toks.append(sc.dma("act", dap, r[:], reads=allk, writes=[dkey]))
                for (dap, dkey) in outs_b:
                    sc.dma("act", dap, yb[:], reads=[("yb", kc) for kc in range(KC)], writes=[dkey])
                if on_done is not None:
                    on_done(toks)

            return [(lambda kc=kc: chunk(kc)) for kc in range(KC)] + [fin]

        final_tokens = []

        for l in range(n_layers):
            g1, b1, g2, b2 = 2 + 4 * l, 3 + 4 * l, 4 + 4 * l, 5 + 4 * l
            lay = ExitStack()
            oT_all = lay.enter_context(nc.sbuf_tensor("oT_all%d" % l, [128, 2, S], BF16))
            with ExitStack() as ph:
                def psb(name, shape, dt=F32):
                    return ph.enter_context(nc.sbuf_tensor(uname(name), list(shape), dt))
                QT = [psb("QT%d" % g, [128, S], BF16) for g in range(3)]
                KT = [psb("KT%d" % g, [128, S], BF16) for g in range(3)]
                VT = [psb("VT%d" % g, [128, S], BF16) for g in range(3)]
                Vtok = [psb("Vtok%d" % g, [128, 32, 128], BF16) for g in range(3)]
                accU = psb("accU", [128, S])
                accZ = psb("accZ", [128, S])
                wqkv = psb("wqkv", [128, KC, 9, 128], BF16)
                hb = psb("hbA", [128, 2, KC, TB], BF16)
                cosb = psb("cosb", [128, TB])
                sinb = psb("sinb", [128, TB])
                qf = psb("qf", [128, 2, TB])
                t1 = psb("t1", [128, 1, TB])
                t2 = psb("t2", [128, 1, TB])
                Pm = psb("Pm", [128, 2, 2, 384], BF16)
                nblk = [0]
                pendC = []
                ln0_r = accU[:].rearrange("p (k t) -> p k t", k=KC)
                ln0_tmp = dict(sq=accZ[:, 0:2 * TB].rearrange("p (a t) -> p a t", a=2), mean=accZ[:, 2 * TB:3 * TB], var=accZ[:, 3 * TB:4 * TB],
                               acs=accZ[:, 4 * TB:5 * TB], acq=accZ[:, 5 * TB:6 * TB],
                               yb=Vtok[0][:].rearrange("p a b -> p (a b)").rearrange("p (k t) -> p k t", k=KC))

                def ln0_steps(b):
                    bs = slice(b * TB, (b + 1) * TB)
                    steps = [lambda: sc.dma("act", ln0_r, fm(xT)[:, :, bs], writes=[("r0", kc) for kc in range(KC)])]
                    steps += [(lambda kc=kc: ln_acc(ln0_r, "r0", kc, ln0_tmp, eng="dve")) for kc in range(KC)]
                    steps.append(lambda: ln_stats(ln0_r, "r0", ln0_tmp))
                    outs = [(fm(hA)[:, :, bs], ("hA", b))]
                    if dump:
                        outs.append((fm(dbg["d_h0"])[:, :, bs], ("d_h0", b)))
                    steps += ln_apply_steps(ln0_r, "r0", 0, 1, outs, [(fm(hAb)[:, :, bs], ("hAb", b))], ln0_tmp,
                                            on_done=(lambda toks: final_tokens.append(toks[-1])) if dump else None, cast_eng="act")
                    return steps

                for pp in range(2):
                    wv = wb["w_in"][l].rearrange("(k p) n -> p k n", p=128)
                    for kind in range(3):
                        for g in range(3):
                            c0 = 768 * kind + 256 * g + 128 * pp
                            sc.dma("sp", wqkv[:, :, 3 * kind + g, :], wv[:, :, c0:c0 + 128],
                                   reads=[("wb", "w_in", l)], writes=[("wqkv", 3 * kind + g)])
                    n_evac = 0
                    n_rot = 0
                    pend = None

                    def rope_post(item):
                        (j, jr, dst, dkey, d) = item
                        b2k = banks[2 + jr]
                        b2key = "bank%d" % (2 + jr)
                        sc.op("pe", lambda e: e.matmul(b2k[:], rmat[:], qf[:, j, :], start=True, stop=True),
                              reads=[("qf", j), "rmat"], writes=[b2key])
                        sc.op("dve", lambda e: e.tensor_tensor(out=t1[:, 0, :], in0=qf[:, j, :], in1=cosb[:], op=ALU.mult),
                              reads=[("qf", j), "cosb"], writes=["t1"])
                        sc.op("dve", lambda e: e.tensor_tensor(out=t2[:, 0, :], in0=b2k[:], in1=sinb[:], op=ALU.mult),
                              reads=[b2key, "sinb"], writes=["t2"])
                        s1 = t1[:, 0, :].rearrange("p (m r) -> p r m", r=d)
                        s2 = t2[:, 0, :].rearrange("p (m r) -> p r m", r=d)
                        sc.op("dve", lambda e: e.tensor_tensor(out=dst, in0=s1, in1=s2, op=ALU.add),
                              reads=["t1", "t2"], writes=[dkey])

                    for b in range(NB):
                        bs = slice(b * TB, (b + 1) * TB)
                        hs = nblk[0] % 2
                        nblk[0] += 1
                        fuse0 = (l == 0 and pp == 0)
                        ln0_pend = []
                        if b == 0 and pp == 0:
                            if fuse0:
                                for st_ in ln0_steps(0):
                                    st_()
                            sc.dma("act", hb[:, hs], fm(hAb)[:, :, bs], reads=[("hAb", 0)], writes=[("hb", hs)])
                        if b + 1 < NB:
                            if fuse0:
                                ln0_pend = ln0_steps(b + 1)
                            else:
                                sc.dma("act", hb[:, 1 - hs], fm(hAb)[:, :, (b + 1) * TB:(b + 2) * TB], reads=[("hAb", b + 1)], writes=[("hb", 1 - hs)])
                        if pend is not None:
                            rope_post(pend)
                            pend = None
                        sc.dma("act", cosb[:], cosd[:, bs], writes=["cosb"])
                        sc.dma("act", sinb[:], sind[:, bs], writes=["sinb"])
                        for g in range(3):
                            d = DIL[g]
                            m0 = b * TB // d
                            mlen = TB // d
                            for kind in range(3):
                                j = n_evac % 2
                                n_evac += 1
                                bk = banks[j]
                                bkey = "bank%d" % j
                                for kc in range(KC):
                                    sc.op("pe", lambda e, kc=kc, bk=bk, wi=3 * kind + g: e.matmul(
                                        bk[:], wqkv[:, kc, wi, :], hb[:, hs, kc, :], start=(kc == 0), stop=(kc == KC - 1)),
                                        reads=[("wqkv", 3 * kind + g), ("hb", hs)], writes=[bkey])
                                for _q in range(3):
                                    if ln0_pend:
                                        ln0_pend.pop(0)()
                                dst_t = (QT, KT, VT)[kind][g]
                                dst = dst_t[:].rearrange("p (r m) -> p r m", r=d)[:, :, m0:m0 + mlen]
                                dkey = ("qkv", kind, g)
                                if kind == 2:
                                    src = bk[:].rearrange("p (m r) -> p r m", r=d)
                                    sc.op("act", lambda e, dst=dst, src=src: e.activation(out=dst, in_=src, func=AF.Copy),
                                          reads=[bkey], writes=[dkey])
                                else:
                                    jq = n_rot % 2
                                    n_rot += 1
                                    sc.op("act", lambda e, jq=jq, bk=bk: e.activation(out=qf[:, jq, :], in_=bk[:], func=AF.Copy),
                                          reads=[bkey], writes=[("qf", jq)])
                                    if pend is not None:
                                        rope_post(pend)
                                    pend = (jq, jq, dst, dkey, d)
                        while ln0_pend:
                            ln0_pend.pop(0)()
                        if fuse0 and b + 1 < NB:
                            sc.dma("act", hb[:, 1 - hs], fm(hAb)[:, :, (b + 1) * TB:(b + 2) * TB], reads=[("hAb", b + 1)], writes=[("hb", 1 - hs)])
                        if pendC:
                            pendC.pop(0)()
                        if l == 0:
                            issue_early_cast([("hb", hs)])
                    if pend is not None:
                        rope_post(pend)
                        pend = None
                    if l == 0 and pp == 0:
                        sc.barrier()
                    if pp == 0:
                        sc.dma("act", hb[:, nblk[0] % 2], fm(hAb)[:, :, 0:TB], reads=[("hAb", 0)], writes=[("hb", nblk[0] % 2)])
                    for g in (range(3) if stop != "A" else ()):
                        for c4 in range(8):
                            for i in range(4):
                                c = 4 * c4 + i
                                sc.op("pe", lambda e, g=g, c=c, i=i: e.transpose(
                                    out=bankT[:, 128 * i:128 * i + 128], in_=VT[g][:, 128 * c:128 * c + 128], identity=ident_b[:]),
                                    reads=[("qkv", 2, g), "ident_b"], writes=["bank7"])
                            sc.op("act", lambda e, g=g, c4=c4: e.activation(
                                out=Vtok[g][:, 4 * c4:4 * c4 + 4, :].rearrange("p a b -> p (a b)"), in_=bankT[:, 0:512], func=AF.Copy),
                                reads=["bank7"], writes=[("vtok", g)])
                    units = []
                    for g in (range(3) if stop not in ("A", "A2") else ()):
                        d = DIL[g]
                        cps = (S // d) // 128
                        for qb in range(32):
                            units.append((g, d, cps, qb))

                    def stage1(ui):
                        (g, d, cps, qb) = units[ui]
                        jj = qb % cps
                        tiles = [t for t in range(3) if 0 <= jj + t - 1 < cps]
                        lo, hi = 128 * tiles[0], 128 * tiles[-1] + 128
                        u = ui % 2
                        for t in tiles:
                            kcn = qb + t - 1
                            for hh in range(2):
                                bk = banks[2 * u + hh]
                                ps_ = slice(64 * hh, 64 * hh + 64)
                                sc.op("pe", lambda e, bk=bk, ps_=ps_: e.matmul(
                                    bk[:, 128 * t:128 * t + 128], KT[g][ps_, 128 * kcn:128 * kcn + 128],
                                    QT[g][ps_, 128 * qb:128 * qb + 128], start=True, stop=True),
                                    reads=[("qkv", 0, g), ("qkv", 1, g)], writes=["bank%d" % (2 * u + hh)])
                        for hh in range(2):
                            bk = banks[2 * u + hh]
                            bkey = "bank%d" % (2 * u + hh)
                            sc.op("act", lambda e, bk=bk, hh=hh: e.activation(
                                out=Pm[:, u, hh, lo:hi], in_=bk[:, lo:hi], func=AF.Exp, scale=0.125),
                                reads=[bkey], writes=[("Pm", u)])
                        sc.op("dve", lambda e: e.tensor_tensor(
                            out=Pm[:, u, :, lo:hi], in0=Pm[:, u, :, lo:hi], in1=mask_b[:, :, lo:hi], op=ALU.mult),
                            reads=[("Pm", u), "mask_b"], writes=[("Pm", u)])

                    def stage2(ui):
                        (g, d, cps, qb) = units[ui]
                        jj = qb % cps
                        rr = qb // cps
                        tiles = [t for t in range(3) if 0 <= jj + t - 1 < cps]
                        u = ui % 2
                        bu = banks[4 + u]
                        bukey = "bank%d" % (4 + u)
                        for zz in range(2):
                            for ti, t in enumerate(tiles):
                                kcn = qb + t - 1
                                for hh in range(2):
                                    ps_ = slice(64 * hh, 64 * hh + 64)
                                    lh = Vtok[g][:, kcn, 64 * hh:64 * hh + 64] if zz == 0 else ones_b[:]
                                    sc.op("pe", lambda e, ps_=ps_, lh=lh, hh=hh: e.matmul(
                                        bu[ps_, 128 * zz:128 * zz + 128], lh, Pm[:, u, hh, 128 * t:128 * t + 128],
                                        start=(ti == 0), stop=(ti == len(tiles) - 1)),
                                        reads=[("Pm", u), ("vtok", g), "ones_b"], writes=[bukey])
                        t0 = rr + 128 * jj * d
                        nat = slice(t0, t0 + 127 * d + 1, d)
                        if g == 0:
                            sc.op("dve", lambda e: e.tensor_copy(out=accU[:, nat], in_=bu[:, 0:128]), reads=[bukey], writes=["accU"])
                            sc.op("dve", lambda e: e.tensor_copy(out=accZ[:, nat], in_=bu[:, 128:256]), reads=[bukey], writes=["accZ"])
                        else:
                            sc.op("dve", lambda e: e.tensor_tensor(out=accU[:, nat], in0=bu[:, 0:128], in1=accU[:, nat], op=ALU.add),
                                  reads=[bukey, "accU"], writes=["accU"])
                            sc.op("dve", lambda e: e.tensor_tensor(out=accZ[:, nat], in0=bu[:, 128:256], in1=accZ[:, nat], op=ALU.add),
                                  reads=[bukey, "accZ"], writes=["accZ"])

                    for ui in range(len(units) + 1):
                        if ui < len(units):
                            stage1(ui)
                        if ui >= 1:
                            stage2(ui - 1)
                    def phase_c(b, pp=pp):
                        bs = slice(b * TB, (b + 1) * TB)
                        sc.op("dve", lambda e: e.reciprocal(out=accZ[:, bs], in_=accZ[:, bs]), reads=["accZ"], writes=["accZ"])
                        sc.op("dve", lambda e: e.tensor_tensor(out=accU[:, bs], in0=accU[:, bs], in1=accZ[:, bs], op=ALU.mult),
                              reads=["accU", "accZ"], writes=["accU"])
                        sc.op("act", lambda e: e.activation(out=oT_all[:, pp, bs], in_=accU[:, bs], func=AF.Copy),
                              reads=["accU"], writes=[("oT", l)])
                        if dump and l == 0:
                            final_tokens.append(sc.dma("act", dbg["d_o"][:, pp * S + b * TB: pp * S + (b + 1) * TB], accU[:, bs],
                                                       reads=["accU"], writes=[("d_o", pp, b)]))
                    if pp == 0:
                        pendC.extend((lambda b=b, f=phase_c: f(b)) for b in range(NB))
                    else:
                        for b in range(NB):
                            phase_c(b)
                sc.barrier()

            while early_casts:
                issue_early_cast([])
            with ExitStack() as ph:
                def psb(name, shape, dt=F32):
                    return ph.enter_context(nc.sbuf_tensor(uname(name), list(shape), dt))
                hf = psb("hfD", [128, KC, TB])
                hb2 = psb("hbD", [128, 2, KC, TB], BF16)
                hh2 = psb("hbhD", [128, 2, KC, 16], BF16)
                wr = psb("wrD", [128, 4, KC, 512], BF16)
                wao = psb("wao", [128, 2, 1024], BF16)
                wpo = psb("wpo", [128, 4, 1024], BF16)
                wpl = psb("wpl", [128, 4, 128], BF16)
                cT = psb("cT", [128, 4, 528])
                edg = psb("edg", [128, 8])
                sA = psb("sA", [128, 528])
                sB = psb("sB", [128, 528])
                pg = psb("pg", [128, 4, TB], BF16)
                mixedT = psb("mixedT", [128, 4, TB], BF16)
                sga = psb("sga", [128, 2, TB])
                sgb = psb("sgb", [128, 2, TB])
                mT = psb("mT", [128, KC, TB], BF16)
                r2 = [psb("rDa", [128, KC, TB]), psb("rDb", [128, KC, TB])]
                tmp = dict(sq=psb("sqD", [128, 2, TB]), mean=psb("meanD", [128, TB]), var=psb("varD", [128, TB]),
                           yb=psb("ybD", [128, KC, TB], BF16), acs=psb("acsD", [128, TB]), acq=psb("acqD", [128, TB]))
                sc.dma("sp", wao[:], wb["w_attn_out"][l].rearrange("(k p) n -> p k n", p=128), reads=[("wb", "w_attn_out", l)], writes=["wao"])
                sc.dma("sp", wpo[:], wb["w_pool_out"][l].rearrange("(k p) n -> p k n", p=128), reads=[("wb", "w_pool_out", l)], writes=["wpo"])
                sc.dma("sp", wpl[:], wb["w_pool"][l].rearrange("(k p) n -> p k n", p=128), reads=[("wb", "w_pool", l)], writes=["wpl"])
                wv = wb["w_in"][l].rearrange("(k p) n -> p k n", p=128)
                wov = wb["w_o"][l].rearrange("(k p) n -> p k n", p=128)
                nring = [0]

                def wload(wname, c0, width=512):
                    s_ = nring[0] % 4
                    nring[0] += 1
                    view = wb[wname][l].rearrange("(k p) n -> p k n", p=128)
                    sc.dma("sp", wr[:, s_, :, 0:width], view[:, :, c0:c0 + width], reads=[("wb", wname, l)], writes=[("wr", s_)])
                    return s_


                def load_block(b):
                    sl = b % 2
                    sc.dma("act", hb2[:, sl], fm(hAb)[:, :, b * TB:(b + 1) * TB], reads=[("hAb", b)], writes=[("hb", sl)])
                    if b > 0:
                        sc.dma("act", hh2[:, sl, :, 0:8], fm(hAb)[:, :, b * TB - 8:b * TB], reads=[("hAb", b - 1)], writes=[("hh0", sl)])
                    else:
                        sc.op("pool", lambda e: e.memset(hh2[:, sl, :, 0:8], 0.0), writes=[("hh0", sl)])
                    if b < NB - 1:
                        sc.dma("act", hh2[:, sl, :, 8:16], fm(hAb)[:, :, (b + 1) * TB:(b + 1) * TB + 8], reads=[("hAb", b + 1)], writes=[("hh1", sl)])
                    else:
                        sc.op("pool", lambda e: e.memset(hh2[:, sl, :, 8:16], 0.0), writes=[("hh1", sl)])

                runD = stop not in ("A", "A2", "B", "C")
                pend_steps = []
                if runD:
                    load_block(0)
                for b in (range(NB) if runD else ()):
                    bs = slice(b * TB, (b + 1) * TB)
                    sl = b % 2
                    if b + 1 < NB:
                        load_block(b + 1)
                    issue_late_cast()
                    sc.dma("act", hf[:], fm(hA)[:, :, bs], reads=[("hA", b)], writes=["hf"])
                    hb = hb2[:, sl]
                    hbh = hh2[:, sl]
                    r = r2[sl]
                    rk = "r%d" % sl
                    hbk = ("hb", sl)
                    hhk = [("hh0", sl), ("hh1", sl)]
                    s_c = wload("w_in", 2304)
                    for gi in range(4):
                        bk, bkey = B8[gi], "bank%d" % gi
                        bh, bhkey = B8[4 + gi], "bank%d" % (4 + gi)
                        for kc in range(KC):
                            sc.op("pe", lambda e, kc=kc: e.matmul(bk[:], wr[:, s_c, kc, 128 * gi:128 * gi + 128], hb[:, kc, :],
                                                                  start=(kc == 0), stop=(kc == KC - 1)),
                                  reads=[("wr", s_c), hbk], writes=[bkey])
                        for kc in range(KC):
                            sc.op("pe", lambda e, kc=kc: e.matmul(bh[:, 0:16], wr[:, s_c, kc, 128 * gi:128 * gi + 128], hbh[:, kc, :],
                                                                  start=(kc == 0), stop=(kc == KC - 1)),
                                  reads=[("wr", s_c)] + hhk, writes=[bhkey])
                        ck = ("cT", gi)
                        sc.op("act", lambda e: e.activation(out=cT[:, gi, 8:520], in_=bk[:], func=AF.Copy), reads=[bkey], writes=[ck])
                        sc.op("act", lambda e: e.activation(out=cT[:, gi, 0:8], in_=bh[:, 0:8], func=AF.Copy), reads=[bhkey], writes=[ck])
                        sc.op("act", lambda e: e.activation(out=cT[:, gi, 520:528], in_=bh[:, 8:16], func=AF.Copy), reads=[bhkey], writes=[ck])
                    for gi in range(4):
                        w = 2 << gi
                        ck = ("cT", gi)
                        src, skey = cT[:, gi, :], ck
                        dsts = [(sA, "sA"), (sB, "sB")]
                        step = 1
                        n = 528
                        di = 0
                        while step < w:
                            n2 = n - step
                            dt_, dk_ = dsts[di % 2]
                            di += 1
                            sc.op("dve", lambda e: e.tensor_tensor(out=dt_[:, 0:n2], in0=src[:, 0:n2], in1=src[:, step:step + n2], op=ALU.add),
                                  reads=[skey], writes=[dk_])
                            src, skey, n, step = dt_[:, :], dk_, n2, step * 2
                        st = 8 - w // 2
                        ot_, ok_ = dsts[di % 2]
                        sc.op("dve", lambda e: e.scalar_tensor_tensor(out=ot_[:, 0:TB], in0=src[:, st:st + TB], scalar=1.0 / w,
                                                                      in1=cT[:, gi, 8:520], op0=ALU.mult, op1=ALU.subtract),
                              reads=[skey, ck], writes=[ok_])
                        if b == 0 or b == NB - 1:
                            e0 = 0 if b == 0 else TB - 8
                            side = 0 if b == 0 else 1
                            sc.op("pool", lambda e: e.tensor_tensor(out=edg[:, 0:8], in0=src[:, st + e0:st + e0 + 8], in1=icnt[:, side, gi, :], op=ALU.mult),
                                  reads=[skey, "icnt"], writes=["edg"])
                            sc.op("pool", lambda e: e.tensor_tensor(out=ot_[:, e0:e0 + 8], in0=edg[:, 0:8], in1=cT[:, gi, 8 + e0:16 + e0], op=ALU.subtract),
                                  reads=["edg", ck, ok_], writes=[ok_])
                        sc.op("act", lambda e: e.activation(out=pg[:, gi, :], in_=ot_[:, 0:TB], func=AF.Copy), reads=[ok_], writes=[("pg", gi)])
                    gw = {}

                    def stage_a(oc):
                        if oc % 4 == 0:
                            gw["ga"] = wload("w_in", 2816 + 512 * (oc // 4))
                            gw["gb"] = wload("w_in", 3840 + 512 * (oc // 4))
                        s_ga, s_gb = gw["ga"], gw["gb"]
                        o4 = (oc % 4) * 128
                        ocs = slice(128 * oc, 128 * oc + 128)
                        ids = (0, 1, 2, 3) if oc % 2 == 0 else (4, 5, 6, 7)
                        Y0, Y1, Y2 = (B8[q] for q in ids[:3])
                        y0, y1, y2 = ("bank%d" % q for q in ids[:3])
                        for kc in range(KC):
                            sc.op("pe", lambda e, kc=kc: e.matmul(Y0[:], wr[:, s_ga, kc, o4:o4 + 128], hb[:, kc, :],
                                                                  start=(kc == 0), stop=(kc == KC - 1)),
                                  reads=[("wr", s_ga), hbk], writes=[y0])
                        for kc in range(KC):
                            sc.op("pe", lambda e, kc=kc: e.matmul(Y1[:], wr[:, s_gb, kc, o4:o4 + 128], hb[:, kc, :],
                                                                  start=(kc == 0), stop=(kc == KC - 1)),
                                  reads=[("wr", s_gb), hbk], writes=[y1])
                        for k2 in range(2):
                            sc.op("pe", lambda e, k2=k2: e.matmul(Y2[:], wao[:, k2, ocs], oT_all[:, k2, bs], start=(k2 == 0), stop=(k2 == 1)),
                                  reads=["wao", ("oT", l)], writes=[y2])
                        st_ = oc % 2
                        sc.op("act", lambda e: e.activation(out=sga[:, st_, :], in_=Y0[:], func=AF.Sigmoid), reads=[y0], writes=[("sga", st_)])
                        sc.op("act", lambda e: e.activation(out=sgb[:, st_, :], in_=Y1[:], func=AF.Sigmoid), reads=[y1], writes=[("sgb", st_)])
                        sc.op("dve", lambda e: e.tensor_tensor(out=sga[:, st_, :], in0=Y2[:], in1=sga[:, st_, :], op=ALU.mult),
                              reads=[y2, ("sga", st_)], writes=[("sga", st_)])

                    def stage_b(oc):
                        ocs = slice(128 * oc, 128 * oc + 128)
                        st_ = oc % 2
                        Y3, y3 = (B8[3], "bank3") if st_ == 0 else (B8[7], "bank7")
                        for gi in range(4):
                            sc.op("pe", lambda e, gi=gi: e.matmul(Y3[:], wpo[:, gi, ocs], mixedT[:, gi, :], start=(gi == 0), stop=(gi == 3)),
                                  reads=["wpo", ("mixedT", gi)], writes=[y3])
                        sc.op("dve", lambda e: e.tensor_tensor(out=sgb[:, st_, :], in0=Y3[:], in1=sgb[:, st_, :], op=ALU.mult),
                              reads=[y3, ("sgb", st_)], writes=[("sgb", st_)])
                        sc.op("dve", lambda e: e.tensor_tensor(out=mT[:, oc, :], in0=sga[:, st_, :], in1=sgb[:, st_, :], op=ALU.add),
                              reads=[("sga", st_), ("sgb", st_)], writes=[("mT", oc)])

                    def mixing():
                        for gi in range(4):
                            bm, bmkey = (B8[3], "bank3") if gi % 2 == 0 else (B8[7], "bank7")
                            sc.op("pe", lambda e: e.matmul(bm[:], wpl[:, gi, :], pg[:, gi, :], start=True, stop=True),
                                  reads=["wpl", ("pg", gi)], writes=[bmkey])
                            sc.op("act", lambda e: e.activation(out=mixedT[:, gi, :], in_=bm[:], func=AF.Identity, scale=psc[:, l, gi:gi + 1]),
                                  reads=[bmkey, "psc"], writes=[("mixedT", gi)])

                    stage_a(0)
                    if pend_steps:
                        pend_steps.pop(0)()
                    stage_a(1)
                    mixing()
                    for oc in range(KC):
                        stage_b(oc)
                        if oc + 2 < KC:
                            stage_a(oc + 2)
                        if pend_steps:
                            pend_steps.pop(0)()
                    while pend_steps:
                        pend_steps.pop(0)()
                    for oc in range(KC):
                        if oc % 4 == 0:
                            s_o = wload("w_o", 512 * (oc // 4))
                        o4 = (oc % 4) * 128
                        bk, bkey = banks[oc % 2], "bank%d" % (oc % 2)
                        for kc in range(KC):
                            sc.op("pe", lambda e, kc=kc, o4=o4, s_o=s_o, bk=bk: e.matmul(bk[:], wr[:, s_o, kc, o4:o4 + 128], mT[:, kc, :],
                                                                                    start=(kc == 0), stop=(kc == KC - 1)),
                                  reads=[("wr", s_o), ("mT", kc)], writes=[bkey])
                        sc.op("dve", lambda e, oc=oc, bk=bk: e.scalar_tensor_tensor(out=r[:, oc, :], in0=hf[:, oc, :], scalar=ALPHA, in1=bk[:],
                                                                                   op0=ALU.mult, op1=ALU.add),
                              reads=["hf", bkey], writes=[(rk, oc)])
                        ln_acc(r, rk, oc, tmp)

                    outs = [(fm(hB)[:, :, bs], ("hB", b))]
                    if dump and l == 0:
                        outs.append((fm(dbg["d_h1"])[:, :, bs], ("d_h1", b)))
                    pend_steps = [lambda r=r, rk=rk: ln_stats(r, rk, tmp)]
                    pend_steps += ln_apply_steps(r, rk, g1, b1, outs, [(fm(hBb)[:, :, bs], ("hBb", b))], tmp,
                                                 cast_eng=("act" if b == NB - 1 else "pool"),
                                                on_done=(lambda toks: final_tokens.append(toks[-1])) if (dump and l == 0) else None)
                while pend_steps:
                    pend_steps.pop(0)()
                sc.barrier()

            with ExitStack() as ph:
                def psb(name, shape, dt=F32):
                    return ph.enter_context(nc.sbuf_tensor(uname(name), list(shape), dt))
                hf = psb("hfE", [128, KC, TB])
                hb2 = psb("hbE", [128, 2, KC, TB], BF16)
                hh2 = psb("hbhE", [128, 2, KC, 16], BF16)
                pf = psb("pfE", [128, 2, TB])
                pb = psb("pbE", [128, 2, TB], BF16)
                wr = psb("wrE", [128, 6, KC, 512], BF16)
                wple = psb("wple", [128, 2, 1024], BF16)
                aext = psb("aext", [128, 3, 514])
                yv = psb("yv", [128, 2, TB])
                yg = psb("yg", [128, 2, TB])
                gT = psb("gT", [128, NFF, TB], BF16)
                sg = psb("sgE", [128, 2, TB])
                r = psb("rE", [128, KC, TB])
                tmp = dict(sq=psb("sqE", [128, 2, TB]), mean=psb("meanE", [128, TB]), var=psb("varE", [128, TB]),
                           yb=psb("ybE", [128, KC, TB], BF16), acs=psb("acsE", [128, TB]), acq=psb("acqE", [128, TB]))
                sc.dma("sp", wple[:], wb["w_ple"][l].rearrange("(k p) n -> p k n", p=128), reads=[("wb", "w_ple", l)], writes=["wple"])
                wuv = wb["w_up"][l].rearrange("(k p) n -> p k n", p=128)
                wdv = wb["w_down"][l].rearrange("(k p) n -> p k n", p=128)
                wgv = wb["w_ple_gate"][l].rearrange("(k p) n -> p k n", p=128)
                nring = [0]

                def wload(wname, c0, width=512):
                    s_ = nring[0] % 6
                    nring[0] += 1
                    view = wb[wname][l].rearrange("(k p) n -> p k n", p=128)
                    sc.dma("sp", wr[:, s_, :, 0:width], view[:, :, c0:c0 + width], reads=[("wb", wname, l)], writes=[("wr", s_)])
                    return s_

                last = (l == n_layers - 1)

                def load_block(b):
                    sl = b % 2
                    sc.dma("act", hb2[:, sl], fm(hBb)[:, :, b * TB:(b + 1) * TB], reads=[("hBb", b)], writes=[("hb", sl)])
                    if b > 0:
                        sc.dma("act", hh2[:, sl, :, 0:8], fm(hBb)[:, :, b * TB - 8:b * TB], reads=[("hBb", b - 1)], writes=[("hh0", sl)])
                    else:
                        sc.op("pool", lambda e: e.memset(hh2[:, sl, :, 0:8], 0.0), writes=[("hh0", sl)])
                    if b < NB - 1:
                        sc.dma("act", hh2[:, sl, :, 8:16], fm(hBb)[:, :, (b + 1) * TB:(b + 1) * TB + 8], reads=[("hBb", b + 1)], writes=[("hh1", sl)])
                    else:
                        sc.op("pool", lambda e: e.memset(hh2[:, sl, :, 8:16], 0.0), writes=[("hh1", sl)])

                runE = stop not in ("A", "A2", "B", "C", "D")
                pend_steps = []
                if runE:
                    load_block(0)
                for b in (range(NB) if runE else ()):
                    bs = slice(b * TB, (b + 1) * TB)
                    sl = b % 2
                    if b + 1 < NB:
                        load_block(b + 1)
                    issue_late_cast()
                    sc.dma("act", hf[:], fm(hB)[:, :, bs], reads=[("hB", b)], writes=["hf"])
                    hb = hb2[:, sl]
                    hbh = hh2[:, sl, :, 7:9]
                    hbk = ("hb", sl)
                    hhk = [("hh0", sl), ("hh1", sl)]
                    sc.dma("act", pf[:], pT[l].rearrange("(k p) t -> p k t", p=128)[:, :, bs], writes=["pf"])
                    sc.op("pool", lambda e: e.tensor_copy(out=pb[:], in_=pf[:]), reads=["pf"], writes=["pb"])
                    n_a = 0
                    deferred = None
                    for i in range(NFF):
                        if i % 4 == 0:
                            wid = 512 if i + 4 <= NFF else 128 * (NFF - i)
                            s_v = wload("w_up", DFF + 128 * i, wid)
                            s_g = wload("w_up", 128 * i, wid)
                        o4 = (i % 4) * 128
                        jp = i % 2
                        for which in (0, 1):
                            s_w = s_v if which == 0 else s_g
                            ch = (NFF + i) if which == 0 else i
                            jm = n_a % 3
                            jh = (3, 4, 7)[n_a % 3]
                            n_a += 1
                            bk, bkey = B8[jm], "bank%d" % jm
                            bh, bhkey = B8[jh], "bank%d" % jh
                            for kc in range(KC):
                                sc.op("pe", lambda e, kc=kc: e.matmul(bk[:], wr[:, s_w, kc, o4:o4 + 128], hb[:, kc, :],
                                                                      start=(kc == 0), stop=(kc == KC - 1)),
                                      reads=[("wr", s_w), hbk], writes=[bkey])
                            for kc in range(KC):
                                sc.op("pe", lambda e, kc=kc: e.matmul(bh[:, 0:2], wr[:, s_w, kc, o4:o4 + 128], hbh[:, kc, :],
                                                                      start=(kc == 0), stop=(kc == KC - 1)),
                                      reads=[("wr", s_w)] + hhk, writes=[bhkey])
                            yo, yk = (yv, ("yv", jp)) if which == 0 else (yg, ("yg", jp))
                            w0, w1, w2, bb = (cw[:, l, q, ch:ch + 1] for q in range(4))
                            ja = (n_a - 1) % 3
                            ak = ("aext", ja)
                            sc.op("act", lambda e: e.activation(out=aext[:, ja, 1:513], in_=bk[:], func=AF.Copy), reads=[bkey], writes=[ak])
                            sc.op("act", lambda e: e.activation(out=aext[:, ja, 0:514:513], in_=bh[:, 0:2], func=AF.Copy), reads=[bhkey], writes=[ak])
                            if which == 0:
                                sc.op("act", lambda e: e.activation(out=yo[:, jp, :], in_=bk[:], func=AF.Identity, bias=bb, scale=w1),
                                      reads=[bkey, "cw"], writes=[yk])
                            else:
                                sc.op("dve", lambda e: e.tensor_scalar(out=yo[:, jp, :], in0=aext[:, ja, 1:513], scalar1=w1, scalar2=bb,
                                                                       op0=ALU.mult, op1=ALU.add),
                                      reads=[ak, "cw"], writes=[yk])
                            sc.op("dve", lambda e: e.scalar_tensor_tensor(out=yo[:, jp, :], in0=aext[:, ja, 0:512], scalar=w0,
                                                                          in1=yo[:, jp, :], op0=ALU.mult, op1=ALU.add),
                                  reads=[ak, "cw", yk], writes=[yk])
                            sc.op("dve", lambda e: e.scalar_tensor_tensor(out=yo[:, jp, :], in0=aext[:, ja, 2:514], scalar=w2,
                                                                          in1=yo[:, jp, :], op0=ALU.mult, op1=ALU.add),
                                  reads=[ak, "cw", yk], writes=[yk])
                            if deferred is not None and which == 0:
                                deferred()
                                deferred = None
                        def fin(i=i, jp=jp):
                            sc.op("act", lambda e: e.activation(out=yg[:, jp, :], in_=yg[:, jp, :], func=AF.Gelu),
                                  reads=[("yg", jp)], writes=[("yg", jp)])
                            sc.op("pool", lambda e: e.tensor_tensor(out=gT[:, i, :], in0=yg[:, jp, :], in1=yv[:, jp, :], op=ALU.mult),
                                  reads=[("yg", jp), ("yv", jp)], writes=[("gT", i)])
                        deferred = fin
                        if i >= 1 and i % 2 == 1 and pend_steps:
                            pend_steps.pop(0)()
                    deferred()
                    deferred = None
                    while pend_steps:
                        pend_steps.pop(0)()
                    n_o = 0
                    for half in range(2):
                        s_wd = []
                        for piece in range(3):
                            cnt = min(8, NFF - 8 * piece)
                            s_ = nring[0] % 6
                            nring[0] += 1
                            sc.dma("sp", wr[:, s_, 0:cnt, :], wdv[:, 8 * piece:8 * piece + cnt, 512 * half:512 * half + 512],
                                   reads=[("wb", "w_down", l)], writes=[("wr", s_)])
                            s_wd.append(s_)
                        s_pg = wload("w_ple_gate", 512 * half)
                        for o_ in range(4):
                            oc = 4 * half + o_
                            o4 = 128 * o_
                            ocs = slice(128 * oc, 128 * oc + 128)
                            st_ = n_o % 2
                            n_o += 1
                            ids = (0, 1, 2) if st_ == 0 else (3, 4, 7)
                            X0, X1, X2 = (B8[q] for q in ids)
                            k0, k1, k2_ = ("bank%d" % q for q in ids)
                            for i in range(NFF):
                                sc.op("pe", lambda e, i=i, sw=s_wd[i // 8]: e.matmul(X0[:], wr[:, sw, i % 8, o4:o4 + 128], gT[:, i, :],
                                                                                 start=(i == 0), stop=(i == NFF - 1)),
                                      reads=[("wr", s_wd[i // 8]), ("gT", i)], writes=[k0])
                            for k2 in range(2):
                                sc.op("pe", lambda e, k2=k2: e.matmul(X1[:], wple[:, k2, ocs], pb[:, k2, :], start=(k2 == 0), stop=(k2 == 1)),
                                      reads=["wple", "pb"], writes=[k1])
                            for kc in range(KC):
                                sc.op("pe", lambda e, kc=kc: e.matmul(X2[:], wr[:, s_pg, kc, o4:o4 + 128], hb[:, kc, :],
                                                                      start=(kc == 0), stop=(kc == KC - 1)),
                                      reads=[("wr", s_pg), hbk], writes=[k2_])
                            sc.op("act", lambda e: e.activation(out=sg[:, st_, :], in_=X2[:], func=AF.Sigmoid), reads=[k2_], writes=[("sg", st_)])
                            sc.op("dve", lambda e: e.tensor_tensor(out=sg[:, st_, :], in0=X1[:], in1=sg[:, st_, :], op=ALU.mult),
                                  reads=[k1, ("sg", st_)], writes=[("sg", st_)])
                            sc.op("dve", lambda e: e.scalar_tensor_tensor(out=r[:, oc, :], in0=hf[:, oc, :], scalar=ALPHA, in1=X0[:],
                                                                          op0=ALU.mult, op1=ALU.add),
                                  reads=["hf", k0], writes=[("r", oc)])
                            sc.op("dve", lambda e: e.tensor_tensor(out=r[:, oc, :], in0=r[:, oc, :], in1=sg[:, st_, :], op=ALU.add),
                                  reads=[("r", oc), ("sg", st_)], writes=[("r", oc)])
                            ln_acc(r, "r", oc, tmp)

                    if last:
                        outs, outs_b = [(fm(yT)[:, :, bs], ("yT", b))], []
                    else:
                        outs, outs_b = [(fm(hA)[:, :, bs], ("hA", b))], [(fm(hAb)[:, :, bs], ("hAb", b))]
                    if dump and l == 0:
                        outs.append((fm(dbg["d_h2"])[:, :, bs], ("d_h2", b)))

                    def done(toks, last=last):
                        if last:
                            final_tokens.append(toks[0])
                        if dump and l == 0:
                            final_tokens.append(toks[-1])
                    pend_steps = [lambda: ln_stats(r, "r", tmp)]
                    pend_steps += ln_apply_steps(r, "r", g2, b2, outs, outs_b, tmp, on_done=done,
                                                 cast_eng=("act" if b == NB - 1 else "pool"))
                while pend_steps:
                    pend_steps.pop(0)()
                sc.barrier()
            lay.close()
            while late_casts:
                issue_late_cast()

        sc.final_wait([(t[0], t[1], "fin") for t in final_tokens])
        with nc.Block() as block:
            sc.emit(block)
    return nc


def _consts():
    pos = np.arange(S, dtype=np.float32)
    inv_freq = (ROPE_THETA ** (-np.arange(0, 16, 2, dtype=np.float32) / 16.0)).astype(np.float32)
    ang = pos[None, :] * inv_freq[:, None]
    cos, sin = np.cos(ang).astype(np.float32), np.sin(ang).astype(np.float32)
    cosT = np.ones((128, S), np.float32)
    sinT = np.zeros((128, S), np.float32)
    rmat = np.zeros((128, 128), np.float32)
    for p in range(128):
        dd = p % 64
        if dd < 8:
            cosT[p] = cos[dd]
            sinT[p] = -sin[dd]
            rmat[p + 8, p] = 1.0
        elif dd < 16:
            cosT[p] = cos[dd - 8]
            sinT[p] = sin[dd - 8]
            rmat[p - 8, p] = 1.0
    ident = np.eye(128, dtype=np.float32)
    i = np.arange(128)[:, None]
    u = np.arange(128)[None, :]
    m1 = np.concatenate([(np.abs(128 * (t - 1) + i - u) <= 64).astype(np.float32) for t in range(3)], axis=1)
    mask3 = np.concatenate([m1, m1], axis=1)
    icnt = np.zeros((128, 2, 4, 8), np.float32)
    for gi in range(4):
        w = 2 << gi
        for c in range(8):
            for side, tok in ((0, c), (1, S - 8 + c)):
                lo, hi = max(tok - w // 2, 0), min(tok + w // 2, S)
                icnt[:, side, gi, c] = 1.0 / float(hi - lo)
    return dict(cosT=cosT, sinT=sinT, rmat=rmat, ident=ident, mask3=mask3, icnt=icnt)


def _vec(v):
    return np.ascontiguousarray(np.asarray(v, np.float32).reshape(-1, 128).T)


def make_in_maps(inputs, n_cores=8):
    x = np.asarray(inputs["x"], np.float32)
    p = np.asarray(inputs["p"], np.float32)
    shared = dict(_consts())
    for (n, k, c) in WEIGHTS:
        shared[n] = np.ascontiguousarray(np.asarray(inputs[n], np.float32).reshape(-1, 1024))
    lnv = np.zeros((128, 2 + 4 * L, KC), np.float32)
    lnv[:, 0] = _vec(inputs["ln0_g"])
    lnv[:, 1] = _vec(inputs["ln0_b"])
    for l in range(L):
        lnv[:, 2 + 4 * l] = _vec(inputs["ln1_g"][l])
        lnv[:, 3 + 4 * l] = _vec(inputs["ln1_b"][l])
        lnv[:, 4 + 4 * l] = _vec(inputs["ln2_g"][l])
        lnv[:, 5 + 4 * l] = _vec(inputs["ln2_b"][l])
    shared["lnv"] = lnv
    psc = np.zeros((128, L, 4), np.float32)
    cwv = np.zeros((128, L, 4, 2 * NFF), np.float32)
    for l in range(L):
        psc[:, l] = _vec(inputs["pool_scale"][l])
        for j in range(3):
            cwv[:, l, j] = _vec(inputs["conv_w"][l][j])
        cwv[:, l, 3] = _vec(inputs["conv_b"][l])
    shared["pscale"] = psc
    shared["convw"] = cwv
    maps = []
    for c in range(n_cores):
        m = dict(shared)
        m["xT"] = np.ascontiguousarray(x[c].T)
        m["pT"] = np.ascontiguousarray(np.transpose(p[:, c], (0, 2, 1)))
        maps.append(m)
    return maps


def kernel(**inputs):
    nc = build_program()
    maps = make_in_maps(inputs, 8)
    res = run_bass_kernel_spmd(nc, maps, core_ids=list(range(8)))
    out = np.stack([np.ascontiguousarray(np.asarray(r["yT"], np.float32).T) for r in res.results], axis=0)
    return out.astype(np.float32)
```

```python
import math
from contextlib import ExitStack

import numpy as np
import concourse.bass as bass
import concourse.mybir as mybir
from concourse.bass_utils import run_bass_kernel_spmd

F32 = mybir.dt.float32
BF16 = mybir.dt.bfloat16
AF = mybir.ActivationFunctionType
ALU = mybir.AluOpType

S = 4096
D = 1024
KC = 8
TB = 512
NB = S // TB
L = 2
IN_W = 4864
DFF = 2816
NFF = 22
PLE = 256
ALPHA = (2.0 * L) ** 0.25
EPS = 1e-5
DIL = (1, 4, 16)
ROPE_THETA = 500000.0
COMPUTE = ("pe", "act", "dve", "pool")


class _Rec:
    def __getattr__(self, name):
        def f(*a, **k):
            return (name, a, k)
        return f


class Sched:
    NDMA = 24

    def __init__(self, nc, es):
        self.nc = nc
        self.engs = ("pe", "act", "dve", "pool", "sp")
        self.ops = {e: [] for e in self.engs}
        self.cnt = {e: 0 for e in COMPUTE}
        self.sem = {e: es.enter_context(nc.semaphore("c_" + e)) for e in COMPUTE}
        self.nd = {"sp": self.NDMA, "pool": 2, "act": 12}
        self.dsem = {q: [es.enter_context(nc.semaphore("d_%s%d" % (q, i))) for i in range(self.nd[q])]
                     for q in ("sp", "pool", "act")}
        self.ndma = {"sp": 0, "pool": 0, "act": 0}
        self.waited = {e: {} for e in self.engs}
        self.lastw = {}
        self.readers = {}
        self.pending = {e: [] for e in self.engs}
        self.all_dma_tokens = []

    def _deps(self, eng, reads, writes):
        toks = []
        for k in reads:
            t = self.lastw.get(k)
            if t is not None:
                toks.append(t)
        for k in writes:
            t = self.lastw.get(k)
            if t is not None:
                toks.append(t)
            toks.extend(self.readers.get(k, ()))
        toks.extend(self.pending[eng])
        self.pending[eng] = []
        need = {}
        for (sem, val, src) in toks:
            if src == "pe" and eng == "pe":
                continue
            key = id(sem)
            if self.waited[eng].get(key, 0) >= val:
                continue
            if key not in need or need[key][1] < val:
                need[key] = (sem, val)
        for key, (sem, val) in need.items():
            self.waited[eng][key] = val
        return list(need.values())

    def _record(self, tok, reads, writes):
        for k in reads:
            lst = self.readers.setdefault(k, [])
            lst[:] = [t for t in lst if not (t[0] is tok[0])]
            lst.append(tok)
        for k in writes:
            self.lastw[k] = tok
            self.readers[k] = []

    def op(self, eng, fn, reads=(), writes=()):
        waits = self._deps(eng, reads, writes)
        self.cnt[eng] += 1
        tok = (self.sem[eng], self.cnt[eng], eng)
        self.ops[eng].append((waits, fn(_Rec()), self.sem[eng], 1))
        self._record(tok, reads, writes)

    def dma(self, q, out, in_, reads=(), writes=()):
        i = self.ndma[q]
        self.ndma[q] += 1
        sem = self.dsem[q][i % self.nd[q]]
        prev = 16 * (i // self.nd[q])
        val = prev + 16
        waits = self._deps(q, reads, writes)
        if prev > 0 and self.waited[q].get(id(sem), 0) < prev:
            waits.append((sem, prev))
            self.waited[q][id(sem)] = prev
        tok = (sem, val, "dma")
        self.ops[q].append((waits, ("dma_start", (), dict(out=out, in_=in_)), sem, 16))
        self._record(tok, reads, writes)
        self.all_dma_tokens.append(tok)
        return tok

    def barrier(self):
        toks = [(self.sem[e], self.cnt[e], "bar") for e in COMPUTE if self.cnt[e] > 0]
        latest = {}
        pool_sems = set(id(x) for x in self.dsem["pool"])
        for t in self.all_dma_tokens:
            if id(t[0]) in pool_sems:
                continue
            latest[id(t[0])] = t
        toks.extend((t[0], t[1], "bar") for t in latest.values())
        for e in self.engs:
            self.pending[e].extend(toks)
        self.lastw = {k: v for k, v in self.lastw.items() if isinstance(k, tuple) and k[0] == "wb"}
        self.readers = {}

    def final_wait(self, tokens):
        self.pending["sp"].extend(tokens)
        self.ops["sp"].append((self._deps("sp", (), ()), None, None, 0))

    def emit(self, block):
        def runner(e):
            def f(eng):
                for (waits, fn, sem, inc) in self.ops[e]:
                    for (s_, v_) in waits:
                        eng.wait_ge(s_, v_)
                    if fn is not None:
                        getattr(eng, fn[0])(*fn[1], **fn[2]).then_inc(sem, inc)
            return f
        block.tensor(runner("pe"))
        block.scalar(runner("act"))
        block.vector(runner("dve"))
        block.gpsimd(runner("pool"))
        block.sync(runner("sp"))


WEIGHTS = [
    ("w_in", 1024, IN_W), ("w_attn_out", 256, 1024), ("w_pool", 512, 128), ("w_pool_out", 512, 1024),
    ("w_o", 1024, 1024), ("w_up", 1024, 2 * DFF), ("w_down", DFF, 1024), ("w_ple", 256, 1024),
    ("w_ple_gate", 1024, 1024),
]


def build_program(n_layers=L, dump=False, stop=None):
    nc = bass.Bass("TRN2", target_bir_lowering=False)
    es = ExitStack()
    with es:
        def din(name, shape, dt=F32):
            return nc.dram_tensor(name, list(shape), dt, kind="ExternalInput").ap()

        xT = din("xT", [D, S])
        pT = din("pT", [L, PLE, S])
        wf = {n: din(n, [L * k * c // 1024, 1024]) for (n, k, c) in WEIGHTS}
        wb = {n: nc.dram_tensor(n + "_b", [L, k, c], BF16, kind="Internal").ap() for (n, k, c) in WEIGHTS}
        cosd = din("cosT", [128, S])
        sind = din("sinT", [128, S])
        rmat_d = din("rmat", [128, 128])
        ident_d = din("ident", [128, 128])
        mask_d = din("mask3", [128, 768])
        lnv_d = din("lnv", [128, 2 + 4 * L, KC])
        psc_d = din("pscale", [128, L, 4])
        cw_d = din("convw", [128, L, 4, 2 * NFF])
        icnt_d = din("icnt", [128, 2, 4, 8])
        yT = nc.dram_tensor("yT", [D, S], F32, kind="ExternalOutput").ap()
        hA = nc.dram_tensor("hA", [D, S], F32, kind="Internal").ap()
        hB = nc.dram_tensor("hB", [D, S], F32, kind="Internal").ap()
        hAb = nc.dram_tensor("hAb", [D, S], BF16, kind="Internal").ap()
        hBb = nc.dram_tensor("hBb", [D, S], BF16, kind="Internal").ap()
        dbg = {}
        if dump:
            for nm in ("d_h0", "d_h1", "d_h2"):
                dbg[nm] = nc.dram_tensor(nm, [D, S], F32, kind="ExternalOutput").ap()
            dbg["d_o"] = nc.dram_tensor("d_o", [128, 2 * S], F32, kind="ExternalOutput").ap()

        def fm(ap):
            return ap.rearrange("(k p) t -> p k t", p=128)

        sc = Sched(nc, es)
        ucount = [0]

        def uname(n):
            ucount[0] += 1
            return "%s_u%d" % (n, ucount[0])

        def sb(name, shape, dt=F32):
            return es.enter_context(nc.sbuf_tensor(name, list(shape), dt))

        ones_f = sb("ones_f", [128, 128])
        ones_b = sb("ones_b", [128, 64], BF16)
        rmat = sb("rmat_s", [128, 128])
        ident_f = sb("ident_f", [128, 128])
        ident_b = sb("ident_b", [128, 128], BF16)
        mask_f = sb("mask_f", [128, 768])
        mask_b = sb("mask_b", [128, 2, 384], BF16)
        lnv = sb("lnv_s", [128, 2 + 4 * L, KC])
        psc = sb("psc_s", [128, L, 4])
        cw = sb("cw_s", [128, L, 4, 2 * NFF])
        icnt = sb("icnt_s", [128, 2, 4, 8])
        banks = [es.enter_context(nc.psum_tensor("bank%d" % i, [128, 512], F32)) for i in range(7)]
        bankT = es.enter_context(nc.psum_tensor("bankT", [128, 1024], BF16))
        B8 = banks + [bankT.bitcast(F32)]

        sc.op("pool", lambda e: e.memset(ones_f[:], 1.0), writes=["ones_f"])
        sc.op("pool", lambda e: e.memset(ones_b[:], 1.0), writes=["ones_b"])
        early_casts = []
        late_casts = []
        for l_ in range(L):
            for (n, k, c) in WEIGHTS:
                rows = k * c // 1024
                dst = wb[n][l_].rearrange("k c -> (k c)").rearrange("(r e) -> r e", e=1024)
                r0 = 0
                piece = 2048 if (l_ == 0 and n == "w_in") else 512
                while r0 < rows:
                    r1 = min(rows, r0 + piece)
                    args = (dst[r0:r1, :], wf[n][l_ * rows + r0:l_ * rows + r1, :], ("wb", n, l_))
                    if l_ == 0 and n == "w_in":
                        sc.dma("pool", args[0], args[1], writes=[args[2]])
                    elif l_ == 0:
                        early_casts.append(args)
                    else:
                        late_casts.append(args)
                    r0 = r1

        def issue_late_cast(n_=3):
            for _ in range(n_):
                if late_casts:
                    a_ = late_casts.pop(0)
                    sc.dma("pool", a_[0], a_[1], writes=[a_[2]])

        def issue_early_cast(after, n_=2):
            for _ in range(n_):
                if early_casts:
                    a_ = early_casts.pop(0)
                    sc.dma("pool", a_[0], a_[1], reads=after, writes=[a_[2]])

        sc.dma("sp", rmat[:], rmat_d, writes=["rmat"])
        sc.dma("sp", ident_f[:], ident_d, writes=["ident_f"])
        sc.dma("sp", mask_f[:], mask_d, writes=["mask_f"])
        sc.dma("sp", lnv[:], lnv_d, writes=["lnv"])
        sc.dma("sp", psc[:], psc_d, writes=["psc"])
        sc.dma("sp", cw[:], cw_d, writes=["cw"])
        sc.dma("sp", icnt[:], icnt_d, writes=["icnt"])
        sc.op("dve", lambda e: e.tensor_copy(out=ident_b[:], in_=ident_f[:]), reads=["ident_f"], writes=["ident_b"])
        sc.op("dve", lambda e: e.tensor_copy(out=mask_b[:].rearrange("p a b -> p (a b)"), in_=mask_f[:]),
              reads=["mask_f"], writes=["mask_b"])

        def ln_acc(r, rkey, kc, tmp, eng="pool"):
            sq, acs, acq = tmp["sq"], tmp["acs"], tmp["acq"]
            j = kc % 2
            if kc == 0:
                sc.op("act", lambda e: e.activation(out=acq[:], in_=r[:, kc, :], func=AF.Square), reads=[(rkey, kc)], writes=["acq"])
                if eng == "pool":
                    sc.op("pool", lambda e: e.tensor_copy(out=acs[:], in_=r[:, kc, :]), reads=[(rkey, kc)], writes=["acs"])
                else:
                    sc.op("act", lambda e: e.activation(out=acs[:], in_=r[:, kc, :], func=AF.Copy), reads=[(rkey, kc)], writes=["acs"])
            else:
                sc.op("act", lambda e: e.activation(out=sq[:, j, :], in_=r[:, kc, :], func=AF.Square), reads=[(rkey, kc)], writes=[("sq", j)])
                sc.op(eng, lambda e: e.tensor_tensor(out=acs[:], in0=acs[:], in1=r[:, kc, :], op=ALU.add), reads=[(rkey, kc), "acs"], writes=["acs"])
                sc.op(eng, lambda e: e.tensor_tensor(out=acq[:], in0=acq[:], in1=sq[:, j, :], op=ALU.add), reads=[("sq", j), "acq"], writes=["acq"])

        def ln_stats(r, rkey, tmp):
            mean, var, acs, acq = tmp["mean"], tmp["var"], tmp["acs"], tmp["acq"]
            b_sum, b_sq = banks[5], banks[6]
            sc.op("pe", lambda e: e.matmul(b_sum[:], ones_f[:], acs[:], start=True, stop=True), reads=["acs", "ones_f"], writes=["bank5"])
            sc.op("pe", lambda e: e.matmul(b_sq[:], ones_f[:], acq[:], start=True, stop=True), reads=["acq", "ones_f"], writes=["bank6"])
            sc.op("act", lambda e: e.activation(out=mean[:], in_=b_sum[:], func=AF.Copy, scale=1.0 / D),
                  reads=["bank5"], writes=["mean"])
            sc.op("dve", lambda e: e.tensor_tensor(out=var[:], in0=mean[:], in1=mean[:], op=ALU.mult),
                  reads=["mean"], writes=["var"])
            sc.op("dve", lambda e: e.scalar_tensor_tensor(out=var[:], in0=b_sq[:], scalar=1.0 / D, in1=var[:],
                                                          op0=ALU.mult, op1=ALU.subtract),
                  reads=["bank6", "var"], writes=["var"])
            sc.op("dve", lambda e: e.tensor_scalar(out=var[:], in0=var[:], scalar1=EPS, scalar2=None, op0=ALU.add),
                  reads=["var"], writes=["var"])
            sc.op("act", lambda e: e.activation(out=var[:], in_=var[:], func=AF.Sqrt), reads=["var"], writes=["var"])
            sc.op("dve", lambda e: e.reciprocal(out=var[:], in_=var[:]), reads=["var"], writes=["var"])
            sc.op("dve", lambda e: e.scalar_tensor_tensor(out=mean[:], in0=mean[:], scalar=-1.0, in1=var[:],
                                                          op0=ALU.mult, op1=ALU.mult),
                  reads=["mean", "var"], writes=["mean"])

        def ln_apply_steps(r, rkey, gi, bi, outs, outs_b, tmp, on_done=None, cast_eng="pool"):
            mean, var, yb = tmp["mean"], tmp["var"], tmp.get("yb")

            def chunk(kc):
                kk = (rkey, kc)
                if cast_eng == "pool":
                    gsc, bsc = lnv[:, gi, kc:kc + 1], lnv[:, bi, kc:kc + 1]
                    sc.op("pool", lambda e: e.tensor_tensor(out=r[:, kc, :], in0=r[:, kc, :], in1=var[:], op=ALU.mult),
                          reads=[kk, "var"], writes=[kk])
                    sc.op("pool", lambda e: e.tensor_tensor(out=r[:, kc, :], in0=r[:, kc, :], in1=mean[:], op=ALU.add),
                          reads=[kk, "mean"], writes=[kk])
                    if outs_b:
                        sc.op("pool", lambda e: e.tensor_scalar(out=yb[:, kc, :], in0=r[:, kc, :], scalar1=gsc, scalar2=bsc,
                                                                op0=ALU.mult, op1=ALU.add),
                              reads=[kk, "lnv"], writes=[("yb", kc)])
                    sc.op("pool", lambda e: e.tensor_scalar(out=r[:, kc, :], in0=r[:, kc, :], scalar1=gsc, scalar2=bsc,
                                                            op0=ALU.mult, op1=ALU.add),
                          reads=[kk, "lnv"], writes=[kk])
                    return
                sc.op("dve", lambda e: e.tensor_tensor(out=r[:, kc, :], in0=r[:, kc, :], in1=var[:], op=ALU.mult),
                      reads=[kk, "var"], writes=[kk])
                sc.op("dve", lambda e: e.tensor_tensor(out=r[:, kc, :], in0=r[:, kc, :], in1=mean[:], op=ALU.add),
                      reads=[kk, "mean"], writes=[kk])
                sc.op("act", lambda e: e.activation(out=r[:, kc, :], in_=r[:, kc, :], func=AF.Identity,
                                                    bias=lnv[:, bi, kc:kc + 1], scale=lnv[:, gi, kc:kc + 1]),
                      reads=[kk, "lnv"], writes=[kk])
                if outs_b:
                    sc.op("act", lambda e: e.activation(out=yb[:, kc, :], in_=r[:, kc, :], func=AF.Copy), reads=[kk], writes=[("yb", kc)])

            def fin():
                toks = []
                allk = [(rkey, kc) for kc in range(KC)]
                for (dap, dkey) in outs:
                    toks.append(sc.dma("act", dap, r[:], reads=allk, writes=[dkey]))
                for (dap, dkey) in outs_b:
                    sc.dma("act", dap, yb[:], reads=[("yb", kc) for kc in range(KC)], writes=[dkey])
                if on_done is not None:
                    on_done(toks)

            return [(lambda kc=kc: chunk(kc)) for kc in range(KC)] + [fin]

        final_tokens = []

        for l in range(n_layers):
            g1, b1, g2, b2 = 2 + 4 * l, 3 + 4 * l, 4 + 4 * l, 5 + 4 * l
            lay = ExitStack()
            oT_all = lay.enter_context(nc.sbuf_tensor("oT_all%d" % l, [128, 2, S], BF16))
            with ExitStack() as ph:
                def psb(name, shape, dt=F32):
                    return ph.enter_context(nc.sbuf_tensor(uname(name), list(shape), dt))
                QT = [psb("QT%d" % g, [128, S], BF16) for g in range(3)]
                KT = [psb("KT%d" % g, [128, S], BF16) for g in range(3)]
                VT = [psb("VT%d" % g, [128, S], BF16) for g in range(3)]
                Vtok = [psb("Vtok%d" % g, [128, 32, 128], BF16) for g in range(3)]
                accU = psb("accU", [128, S])
                accZ = psb("accZ", [128, S])
                wqkv = psb("wqkv", [128, KC, 9, 128], BF16)
                hb = psb("hbA", [128, 2, KC, TB], BF16)
                cosb = psb("cosb", [128, TB])
                sinb = psb("sinb", [128, TB])
                qf = psb("qf", [128, 2, TB])
                t1 = psb("t1", [128, 1, TB])
                t2 = psb("t2", [128, 1, TB])
                Pm = psb("Pm", [128, 2, 2, 384], BF16)
                nblk = [0]
                pendC = []
                ln0_r = accU[:].rearrange("p (k t) -> p k t", k=KC)
                ln0_tmp = dict(sq=accZ[:, 0:2 * TB].rearrange("p (a t) -> p a t", a=2), mean=accZ[:, 2 * TB:3 * TB], var=accZ[:, 3 * TB:4 * TB],
                               acs=accZ[:, 4 * TB:5 * TB], acq=accZ[:, 5 * TB:6 * TB],
                               yb=Vtok[0][:].rearrange("p a b -> p (a b)").rearrange("p (k t) -> p k t", k=KC))

                def ln0_steps(b):
                    bs = slice(b * TB, (b + 1) * TB)
                    steps = [lambda: sc.dma("act", ln0_r, fm(xT)[:, :, bs], writes=[("r0", kc) for kc in range(KC)])]
                    steps += [(lambda kc=kc: ln_acc(ln0_r, "r0", kc, ln0_tmp, eng="dve")) for kc in range(KC)]
                    steps.append(lambda: ln_stats(ln0_r, "r0", ln0_tmp))
                    outs = [(fm(hA)[:, :, bs], ("hA", b))]
                    if dump:
                        outs.append((fm(dbg["d_h0"])[:, :, bs], ("d_h0", b)))
                    steps += ln_apply_steps(ln0_r, "r0", 0, 1, outs, [(fm(hAb)[:, :, bs], ("hAb", b))], ln0_tmp,
                                            on_done=(lambda toks: final_tokens.append(toks[-1])) if dump else None, cast_eng="act")
                    return steps

                for pp in range(2):
                    wv = wb["w_in"][l].rearrange("(k p) n -> p k n", p=128)
                    for kind in range(3):
                        for g in range(3):
                            c0 = 768 * kind + 256 * g + 128 * pp
                            sc.dma("sp", wqkv[:, :, 3 * kind + g, :], wv[:, :, c0:c0 + 128],
                                   reads=[("wb", "w_in", l)], writes=[("wqkv", 3 * kind + g)])
                    n_evac = 0
                    n_rot = 0
                    pend = None

                    def rope_post(item):
                        (j, jr, dst, dkey, d) = item
                        b2k = banks[2 + jr]
                        b2key = "bank%d" % (2 + jr)
                        sc.op("pe", lambda e: e.matmul(b2k[:], rmat[:], qf[:, j, :], start=True, stop=True),
                              reads=[("qf", j), "rmat"], writes=[b2key])
                        sc.op("dve", lambda e: e.tensor_tensor(out=t1[:, 0, :], in0=qf[:, j, :], in1=cosb[:], op=ALU.mult),
                              reads=[("qf", j), "cosb"], writes=["t1"])
                        sc.op("dve", lambda e: e.tensor_tensor(out=t2[:, 0, :], in0=b2k[:], in1=sinb[:], op=ALU.mult),
                              reads=[b2key, "sinb"], writes=["t2"])
                        s1 = t1[:, 0, :].rearrange("p (m r) -> p r m", r=d)
                        s2 = t2[:, 0, :].rearrange("p (m r) -> p r m", r=d)
                        sc.op("dve", lambda e: e.tensor_tensor(out=dst, in0=s1, in1=s2, op=ALU.add),
                              reads=["t1", "t2"], writes=[dkey])

                    for b in range(NB):
                        bs = slice(b * TB, (b + 1) * TB)
                        hs = nblk[0] % 2
                        nblk[0] += 1
                        fuse0 = (l == 0 and pp == 0)
                        ln0_pend = []
                        if b == 0 and pp == 0:
                            if fuse0:
                                for st_ in ln0_steps(0):
                                    st_()
                            sc.dma("act", hb[:, hs], fm(hAb)[:, :, bs], reads=[("hAb", 0)], writes=[("hb", hs)])
                        if b + 1 < NB:
                            if fuse0:
                                ln0_pend = ln0_steps(b + 1)
                            else:
                                sc.dma("act", hb[:, 1 - hs], fm(hAb)[:, :, (b + 1) * TB:(b + 2) * TB], reads=[("hAb", b + 1)], writes=[("hb", 1 - hs)])
                        if pend is not None:
                            rope_post(pend)
                            pend = None
                        sc.dma("act", cosb[:], cosd[:, bs], writes=["cosb"])
                        sc.dma("act", sinb[:], sind[:, bs], writes=["sinb"])
                        for g in range(3):
                            d = DIL[g]
                            m0 = b * TB // d
                            mlen = TB // d
                            for kind in range(3):
                                j = n_evac % 2
                                n_evac += 1
                                bk = banks[j]
                                bkey = "bank%d" % j
                                for kc in range(KC):
                                    sc.op("pe", lambda e, kc=kc, bk=bk, wi=3 * kind + g: e.matmul(
                                        bk[:], wqkv[:, kc, wi, :], hb[:, hs, kc, :], start=(kc == 0), stop=(kc == KC - 1)),
                                        reads=[("wqkv", 3 * kind + g), ("hb", hs)], writes=[bkey])
                                for _q in range(3):
                                    if ln0_pend:
                                        ln0_pend.pop(0)()
                                dst_t = (QT, KT, VT)[kind][g]
                                dst = dst_t[:].rearrange("p (r m) -> p r m", r=d)[:, :, m0:m0 + mlen]
                                dkey = ("qkv", kind, g)
                                if kind == 2:
                                    src = bk[:].rearrange("p (m r) -> p r m", r=d)
                                    sc.op("act", lambda e, dst=dst, src=src: e.activation(out=dst, in_=src, func=AF.Copy),
                                          reads=[bkey], writes=[dkey])
                                else:
                                    jq = n_rot % 2
                                    n_rot += 1
                                    sc.op("act", lambda e, jq=jq, bk=bk: e.activation(out=qf[:, jq, :], in_=bk[:], func=AF.Copy),
                                          reads=[bkey], writes=[("qf", jq)])
                                    if pend is not None:
                                        rope_post(pend)
                                    pend = (jq, jq, dst, dkey, d)
                        while ln0_pend:
                            ln0_pend.pop(0)()
                        if fuse0 and b + 1 < NB:
                            sc.dma("act", hb[:, 1 - hs], fm(hAb)[:, :, (b + 1) * TB:(b + 2) * TB], reads=[("hAb", b + 1)], writes=[("hb", 1 - hs)])
                        if pendC:
                            pendC.pop(0)()
                        if l == 0:
                            issue_early_cast([("hb", hs)])
                    if pend is not None:
                        rope_post(pend)
                        pend = None
                    if l == 0 and pp == 0:
                        sc.barrier()
                    if pp == 0:
                        sc.dma("act", hb[:, nblk[0] % 2], fm(hAb)[:, :, 0:TB], reads=[("hAb", 0)], writes=[("hb", nblk[0] % 2)])
                    for g in (range(3) if stop != "A" else ()):
                        for c4 in range(8):
                            for i in range(4):
                                c = 4 * c4 + i
                                sc.op("pe", lambda e, g=g, c=c, i=i: e.transpose(
                                    out=bankT[:, 128 * i:128 * i + 128], in_=VT[g][:, 128 * c:128 * c + 128], identity=ident_b[:]),
                                    reads=[("qkv", 2, g), "ident_b"], writes=["bank7"])
                            sc.op("act", lambda e, g=g, c4=c4: e.activation(
                                out=Vtok[g][:, 4 * c4:4 * c4 + 4, :].rearrange("p a b -> p (a b)"), in_=bankT[:, 0:512], func=AF.Copy),
                                reads=["bank7"], writes=[("vtok", g)])
                    units = []
                    for g in (range(3) if stop not in ("A", "A2") else ()):
                        d = DIL[g]
                        cps = (S // d) // 128
                        for qb in range(32):
                            units.append((g, d, cps, qb))

                    def stage1(ui):
                        (g, d, cps, qb) = units[ui]
                        jj = qb % cps
                        tiles = [t for t in range(3) if 0 <= jj + t - 1 < cps]
                        lo, hi = 128 * tiles[0], 128 * tiles[-1] + 128
                        u = ui % 2
                        for t in tiles:
                            kcn = qb + t - 1
                            for hh in range(2):
                                bk = banks[2 * u + hh]
                                ps_ = slice(64 * hh, 64 * hh + 64)
                                sc.op("pe", lambda e, bk=bk, ps_=ps_: e.matmul(
                                    bk[:, 128 * t:128 * t + 128], KT[g][ps_, 128 * kcn:128 * kcn + 128],
                                    QT[g][ps_, 128 * qb:128 * qb + 128], start=True, stop=True),
                                    reads=[("qkv", 0, g), ("qkv", 1, g)], writes=["bank%d" % (2 * u + hh)])
                        for hh in range(2):
                            bk = banks[2 * u + hh]
                            bkey = "bank%d" % (2 * u + hh)
                            sc.op("act", lambda e, bk=bk, hh=hh: e.activation(
                                out=Pm[:, u, hh, lo:hi], in_=bk[:, lo:hi], func=AF.Exp, scale=0.125),
                                reads=[bkey], writes=[("Pm", u)])
                        sc.op("dve", lambda e: e.tensor_tensor(
                            out=Pm[:, u, :, lo:hi], in0=Pm[:, u, :, lo:hi], in1=mask_b[:, :, lo:hi], op=ALU.mult),
                            reads=[("Pm", u), "mask_b"], writes=[("Pm", u)])

                    def stage2(ui):
                        (g, d, cps, qb) = units[ui]
                        jj = qb % cps
                        rr = qb // cps
                        tiles = [t for t in range(3) if 0 <= jj + t - 1 < cps]
                        u = ui % 2
                        bu = banks[4 + u]
                        bukey = "bank%d" % (4 + u)
                        for zz in range(2):
                            for ti, t in enumerate(tiles):
                                kcn = qb + t - 1
                                for hh in range(2):
                                    ps_ = slice(64 * hh, 64 * hh + 64)
                                    lh = Vtok[g][:, kcn, 64 * hh:64 * hh + 64] if zz == 0 else ones_b[:]
                                    sc.op("pe", lambda e, ps_=ps_, lh=lh, hh=hh: e.matmul(
                                        bu[ps_, 128 * zz:128 * zz + 128], lh, Pm[:, u, hh, 128 * t:128 * t + 128],
                                        start=(ti == 0), stop=(ti == len(tiles) - 1)),
                                        reads=[("Pm", u), ("vtok", g), "ones_b"], writes=[bukey])
                        t0 = rr + 128 * jj * d
                        nat = slice(t0, t0 + 127 * d + 1, d)
                        if g == 0:
                            sc.op("dve", lambda e: e.tensor_copy(out=accU[:, nat], in_=bu[:, 0:128]), reads=[bukey], writes=["accU"])
                            sc.op("dve", lambda e: e.tensor_copy(out=accZ[:, nat], in_=bu[:, 128:256]), reads=[bukey], writes=["accZ"])
                        else:
                            sc.op("dve", lambda e: e.tensor_tensor(out=accU[:, nat], in0=bu[:, 0:128], in1=accU[:, nat], op=ALU.add),
                                  reads=[bukey, "accU"], writes=["accU"])
                            sc.op("dve", lambda e: e.tensor_tensor(out=accZ[:, nat], in0=bu[:, 128:256], in1=accZ[:, nat], op=ALU.add),
                                  reads=[bukey, "accZ"], writes=["accZ"])

                    for ui in range(len(units) + 1):
                        if ui < len(units):
                            stage1(ui)
                        if ui >= 1:
                            stage2(ui - 1)
                    def phase_c(b, pp=pp):
                        bs = slice(b * TB, (b + 1) * TB)
                        sc.op("dve", lambda e: e.reciprocal(out=accZ[:, bs], in_=accZ[:, bs]), reads=["accZ"], writes=["accZ"])
                        sc.op("dve", lambda e: e.tensor_tensor(out=accU[:, bs], in0=accU[:, bs], in1=accZ[:, bs], op=ALU.mult),
                              reads=["accU", "accZ"], writes=["accU"])
                        sc.op("act", lambda e: e.activation(out=oT_all[:, pp, bs], in_=accU[:, bs], func=AF.Copy),
                              reads=["accU"], writes=[("oT", l)])
                        if dump and l == 0:
                            final_tokens.append(sc.dma("act", dbg["d_o"][:, pp * S + b * TB: pp * S + (b + 1) * TB], accU[:, bs],
                                                       reads=["accU"], writes=[("d_o", pp, b)]))
                    if pp == 0:
                        pendC.extend((lambda b=b, f=phase_c: f(b)) for b in range(NB))
                    else:
                        for b in range(NB):
                            phase_c(b)
                sc.barrier()

            while early_casts:
                issue_early_cast([])
            with ExitStack() as ph:
                def psb(name, shape, dt=F32):
                    return ph.enter_context(nc.sbuf_tensor(uname(name), list(shape), dt))
                hf = psb("hfD", [128, KC, TB])
                hb2 = psb("hbD", [128, 2, KC, TB], BF16)
                hh2 = psb("hbhD", [128, 2, KC, 16], BF16)
                wr = psb("wrD", [128, 4, KC, 512], BF16)
                wao = psb("wao", [128, 2, 1024], BF16)
                wpo = psb("wpo", [128, 4, 1024], BF16)
                wpl = psb("wpl", [128, 4, 128], BF16)
                cT = psb("cT", [128, 4, 528])
                edg = psb("edg", [128, 8])
                sA = psb("sA", [128, 528])
                sB = psb("sB", [128, 528])
                pg = psb("pg", [128, 4, TB], BF16)
                mixedT = psb("mixedT", [128, 4, TB], BF16)
                sga = psb("sga", [128, 2, TB])
                sgb = psb("sgb", [128, 2, TB])
                mT = psb("mT", [128, KC, TB], BF16)
                r2 = [psb("rDa", [128, KC, TB]), psb("rDb", [128, KC, TB])]
                tmp = dict(sq=psb("sqD", [128, 2, TB]), mean=psb("meanD", [128, TB]), var=psb("varD", [128, TB]),
                           yb=psb("ybD", [128, KC, TB], BF16), acs=psb("acsD", [128, TB]), acq=psb("acqD", [128, TB]))
                sc.dma("sp", wao[:], wb["w_attn_out"][l].rearrange("(k p) n -> p k n", p=128), reads=[("wb", "w_attn_out", l)], writes=["wao"])
                sc.dma("sp", wpo[:], wb["w_pool_out"][l].rearrange("(k p) n -> p k n", p=128), reads=[("wb", "w_pool_out", l)], writes=["wpo"])
                sc.dma("sp", wpl[:], wb["w_pool"][l].rearrange("(k p) n -> p k n", p=128), reads=[("wb", "w_pool", l)], writes=["wpl"])
                wv = wb["w_in"][l].rearrange("(k p) n -> p k n", p=128)
                wov = wb["w_o"][l].rearrange("(k p) n -> p k n", p=128)
                nring = [0]

                def wload(wname, c0, width=512):
                    s_ = nring[0] % 4
                    nring[0] += 1
                    view = wb[wname][l].rearrange("(k p) n -> p k n", p=128)
                    sc.dma("sp", wr[:, s_, :, 0:width], view[:, :, c0:c0 + width], reads=[("wb", wname, l)], writes=[("wr", s_)])
                    return s_


                def load_block(b):
                    sl = b % 2
                    sc.dma("act", hb2[:, sl], fm(hAb)[:, :, b * TB:(b + 1) * TB], reads=[("hAb", b)], writes=[("hb", sl)])
                    if b > 0:
                        sc.dma("act", hh2[:, sl, :, 0:8], fm(hAb)[:, :, b * TB - 8:b * TB], reads=[("hAb", b - 1)], writes=[("hh0", sl)])
                    else:
                        sc.op("pool", lambda e: e.memset(hh2[:, sl, :, 0:8], 0.0), writes=[("hh0", sl)])
                    if b < NB - 1:
                        sc.dma("act", hh2[:, sl, :, 8:16], fm(hAb)[:, :, (b + 1) * TB:(b + 1) * TB + 8], reads=[("hAb", b + 1)], writes=[("hh1", sl)])
                    else:
                        sc.op("pool", lambda e: e.memset(hh2[:, sl, :, 8:16], 0.0), writes=[("hh1", sl)])

                runD = stop not in ("A", "A2", "B", "C")
                pend_steps = []
                if runD:
                    load_block(0)
                for b in (range(NB) if runD else ()):
                    bs = slice(b * TB, (b + 1) * TB)
                    sl = b % 2
                    if b + 1 < NB:
                        load_block(b + 1)
                    issue_late_cast()
                    sc.dma("act", hf[:], fm(hA)[:, :, bs], reads=[("hA", b)], writes=["hf"])
                    hb = hb2[:, sl]
                    hbh = hh2[:, sl]
                    r = r2[sl]
                    rk = "r%d" % sl
                    hbk = ("hb", sl)
                    hhk = [("hh0", sl), ("hh1", sl)]
                    s_c = wload("w_in", 2304)
                    for gi in range(4):
                        bk, bkey = B8[gi], "bank%d" % gi
                        bh, bhkey = B8[4 + gi], "bank%d" % (4 + gi)
                        for kc in range(KC):
                            sc.op("pe", lambda e, kc=kc: e.matmul(bk[:], wr[:, s_c, kc, 128 * gi:128 * gi + 128], hb[:, kc, :],
                                                                  start=(kc == 0), stop=(kc == KC - 1)),
                                  reads=[("wr", s_c), hbk], writes=[bkey])
                        for kc in range(KC):
                            sc.op("pe", lambda e, kc=kc: e.matmul(bh[:, 0:16], wr[:, s_c, kc, 128 * gi:128 * gi + 128], hbh[:, kc, :],
                                                                  start=(kc == 0), stop=(kc == KC - 1)),
                                  reads=[("wr", s_c)] + hhk, writes=[bhkey])
                        ck = ("cT", gi)
                        sc.op("act", lambda e: e.activation(out=cT[:, gi, 8:520], in_=bk[:], func=AF.Copy), reads=[bkey], writes=[ck])
                        sc.op("act", lambda e: e.activation(out=cT[:, gi, 0:8], in_=bh[:, 0:8], func=AF.Copy), reads=[bhkey], writes=[ck])
                        sc.op("act", lambda e: e.activation(out=cT[:, gi, 520:528], in_=bh[:, 8:16], func=AF.Copy), reads=[bhkey], writes=[ck])
                    for gi in range(4):
                        w = 2 << gi
                        ck = ("cT", gi)
                        src, skey = cT[:, gi, :], ck
                        dsts = [(sA, "sA"), (sB, "sB")]
                        step = 1
                        n = 528
                        di = 0
                        while step < w:
                            n2 = n - step
                            dt_, dk_ = dsts[di % 2]
                            di += 1
                            sc.op("dve", lambda e: e.tensor_tensor(out=dt_[:, 0:n2], in0=src[:, 0:n2], in1=src[:, step:step + n2], op=ALU.add),
                                  reads=[skey], writes=[dk_])
                            src, skey, n, step = dt_[:, :], dk_, n2, step * 2
                        st = 8 - w // 2
                        ot_, ok_ = dsts[di % 2]
                        sc.op("dve", lambda e: e.scalar_tensor_tensor(out=ot_[:, 0:TB], in0=src[:, st:st + TB], scalar=1.0 / w,
                                                                      in1=cT[:, gi, 8:520], op0=ALU.mult, op1=ALU.subtract),
                              reads=[skey, ck], writes=[ok_])
                        if b == 0 or b == NB - 1:
                            e0 = 0 if b == 0 else TB - 8
                            side = 0 if b == 0 else 1
                            sc.op("pool", lambda e: e.tensor_tensor(out=edg[:, 0:8], in0=src[:, st + e0:st + e0 + 8], in1=icnt[:, side, gi, :], op=ALU.mult),
                                  reads=[skey, "icnt"], writes=["edg"])
                            sc.op("pool", lambda e: e.tensor_tensor(out=ot_[:, e0:e0 + 8], in0=edg[:, 0:8], in1=cT[:, gi, 8 + e0:16 + e0], op=ALU.subtract),
                                  reads=["edg", ck, ok_], writes=[ok_])
                        sc.op("act", lambda e: e.activation(out=pg[:, gi, :], in_=ot_[:, 0:TB], func=AF.Copy), reads=[ok_], writes=[("pg", gi)])
                    gw = {}

                    def stage_a(oc):
                        if oc % 4 == 0:
                            gw["ga"] = wload("w_in", 2816 + 512 * (oc // 4))
                            gw["gb"] = wload("w_in", 3840 + 512 * (oc // 4))
                        s_ga, s_gb = gw["ga"], gw["gb"]
                        o4 = (oc % 4) * 128
                        ocs = slice(128 * oc, 128 * oc + 128)
                        ids = (0, 1, 2, 3) if oc % 2 == 0 else (4, 5, 6, 7)
                        Y0, Y1, Y2 = (B8[q] for q in ids[:3])
                        y0, y1, y2 = ("bank%d" % q for q in ids[:3])
                        for kc in range(KC):
                            sc.op("pe", lambda e, kc=kc: e.matmul(Y0[:], wr[:, s_ga, kc, o4:o4 + 128], hb[:, kc, :],
                                                                  start=(kc == 0), stop=(kc == KC - 1)),
                                  reads=[("wr", s_ga), hbk], writes=[y0])
                        for kc in range(KC):
                            sc.op("pe", lambda e, kc=kc: e.matmul(Y1[:], wr[:, s_gb, kc, o4:o4 + 128], hb[:, kc, :],
                                                                  start=(kc == 0), stop=(kc == KC - 1)),
                                  reads=[("wr", s_gb), hbk], writes=[y1])
                        for k2 in range(2):
                            sc.op("pe", lambda e, k2=k2: e.matmul(Y2[:], wao[:, k2, ocs], oT_all[:, k2, bs], start=(k2 == 0), stop=(k2 == 1)),
                                  reads=["wao", ("oT", l)], writes=[y2])
                        st_ = oc % 2
                        sc.op("act", lambda e: e.activation(out=sga[:, st_, :], in_=Y0[:], func=AF.Sigmoid), reads=[y0], writes=[("sga", st_)])
                        sc.op("act", lambda e: e.activation(out=sgb[:, st_, :], in_=Y1[:], func=AF.Sigmoid), reads=[y1], writes=[("sgb", st_)])
                        sc.op("dve", lambda e: e.tensor_tensor(out=sga[:, st_, :], in0=Y2[:], in1=sga[:, st_, :], op=ALU.mult),
                              reads=[y2, ("sga", st_)], writes=[("sga", st_)])

                    def stage_b(oc):
                        ocs = slice(128 * oc, 128 * oc + 128)
                        st_ = oc % 2
                        Y3, y3 = (B8[3], "bank3") if st_ == 0 else (B8[7], "bank7")
                        for gi in range(4):
                            sc.op("pe", lambda e, gi=gi: e.matmul(Y3[:], wpo[:, gi, ocs], mixedT[:, gi, :], start=(gi == 0), stop=(gi == 3)),
                                  reads=["wpo", ("mixedT", gi)], writes=[y3])
                        sc.op("dve", lambda e: e.tensor_tensor(out=sgb[:, st_, :], in0=Y3[:], in1=sgb[:, st_, :], op=ALU.mult),
                              reads=[y3, ("sgb", st_)], writes=[("sgb", st_)])
                        sc.op("dve", lambda e: e.tensor_tensor(out=mT[:, oc, :], in0=sga[:, st_, :], in1=sgb[:, st_, :], op=ALU.add),
                              reads=[("sga", st_), ("sgb", st_)], writes=[("mT", oc)])

                    def mixing():
                        for gi in range(4):
                            bm, bmkey = (B8[3], "bank3") if gi % 2 == 0 else (B8[7], "bank7")
                            sc.op("pe", lambda e: e.matmul(bm[:], wpl[:, gi, :], pg[:, gi, :], start=True, stop=True),
                                  reads=["wpl", ("pg", gi)], writes=[bmkey])
                            sc.op("act", lambda e: e.activation(out=mixedT[:, gi, :], in_=bm[:], func=AF.Identity, scale=psc[:, l, gi:gi + 1]),
                                  reads=[bmkey, "psc"], writes=[("mixedT", gi)])

                    stage_a(0)
                    if pend_steps:
                        pend_steps.pop(0)()
                    stage_a(1)
                    mixing()
                    for oc in range(KC):
                        stage_b(oc)
                        if oc + 2 < KC:
                            stage_a(oc + 2)
                        if pend_steps:
                            pend_steps.pop(0)()
                    while pend_steps:
                        pend_steps.pop(0)()
                    for oc in range(KC):
                        if oc % 4 == 0:
                            s_o = wload("w_o", 512 * (oc // 4))
                        o4 = (oc % 4) * 128
                        bk, bkey = banks[oc % 2], "bank%d" % (oc % 2)
                        for kc in range(KC):
                            sc.op("pe", lambda e, kc=kc, o4=o4, s_o=s_o, bk=bk: e.matmul(bk[:], wr[:, s_o, kc, o4:o4 + 128], mT[:, kc, :],
                                                                                    start=(kc == 0), stop=(kc == KC - 1)),
                                  reads=[("wr", s_o), ("mT", kc)], writes=[bkey])
                        sc.op("dve", lambda e, oc=oc, bk=bk: e.scalar_tensor_tensor(out=r[:, oc, :], in0=hf[:, oc, :], scalar=ALPHA, in1=bk[:],
                                                                                   op0=ALU.mult, op1=ALU.add),
                              reads=["hf", bkey], writes=[(rk, oc)])
                        ln_acc(r, rk, oc, tmp)

                    outs = [(fm(hB)[:, :, bs], ("hB", b))]
                    if dump and l == 0:
                        outs.append((fm(dbg["d_h1"])[:, :, bs], ("d_h1", b)))
                    pend_steps = [lambda r=r, rk=rk: ln_stats(r, rk, tmp)]
                    pend_steps += ln_apply_steps(r, rk, g1, b1, outs, [(fm(hBb)[:, :, bs], ("hBb", b))], tmp,
                                                 cast_eng=("act" if b == NB - 1 else "pool"),
                                                on_done=(lambda toks: final_tokens.append(toks[-1])) if (dump and l == 0) else None)
                while pend_steps:
                    pend_steps.pop(0)()
                sc.barrier()

            with ExitStack() as ph:
                def psb(name, shape, dt=F32):
                    return ph.enter_context(nc.sbuf_tensor(uname(name), list(shape), dt))
                hf = psb("hfE", [128, KC, TB])
                hb2 = psb("hbE", [128, 2, KC, TB], BF16)
                hh2 = psb("hbhE", [128, 2, KC, 16], BF16)
                pf = psb("pfE", [128, 2, TB])
                pb = psb("pbE", [128, 2, TB], BF16)
                wr = psb("wrE", [128, 6, KC, 512], BF16)
                wple = psb("wple", [128, 2, 1024], BF16)
                aext = psb("aext", [128, 3, 514])
                yv = psb("yv", [128, 2, TB])
                yg = psb("yg", [128, 2, TB])
                gT = psb("gT", [128, NFF, TB], BF16)
                sg = psb("sgE", [128, 2, TB])
                r = psb("rE", [128, KC, TB])
                tmp = dict(sq=psb("sqE", [128, 2, TB]), mean=psb("meanE", [128, TB]), var=psb("varE", [128, TB]),
                           yb=psb("ybE", [128, KC, TB], BF16), acs=psb("acsE", [128, TB]), acq=psb("acqE", [128, TB]))
                sc.dma("sp", wple[:], wb["w_ple"][l].rearrange("(k p) n -> p k n", p=128), reads=[("wb", "w_ple", l)], writes=["wple"])
                wuv = wb["w_up"][l].rearrange("(k p) n -> p k n", p=128)
                wdv = wb["w_down"][l].rearrange("(k p) n -> p k n", p=128)
                wgv = wb["w_ple_gate"][l].rearrange("(k p) n -> p k n", p=128)
                nring = [0]

                def wload(wname, c0, width=512):
                    s_ = nring[0] % 6
                    nring[0] += 1
                    view = wb[wname][l].rearrange("(k p) n -> p k n", p=128)
                    sc.dma("sp", wr[:, s_, :, 0:width], view[:, :, c0:c0 + width], reads=[("wb", wname, l)], writes=[("wr", s_)])
                    return s_

                last = (l == n_layers - 1)

                def load_block(b):
                    sl = b % 2
                    sc.dma("act", hb2[:, sl], fm(hBb)[:, :, b * TB:(b + 1) * TB], reads=[("hBb", b)], writes=[("hb", sl)])
                    if b > 0:
                        sc.dma("act", hh2[:, sl, :, 0:8], fm(hBb)[:, :, b * TB - 8:b * TB], reads=[("hBb", b - 1)], writes=[("hh0", sl)])
                    else:
                        sc.op("pool", lambda e: e.memset(hh2[:, sl, :, 0:8], 0.0), writes=[("hh0", sl)])
                    if b < NB - 1:
                        sc.dma("act", hh2[:, sl, :, 8:16], fm(hBb)[:, :, (b + 1) * TB:(b + 1) * TB + 8], reads=[("hBb", b + 1)], writes=[("hh1", sl)])
                    else:
                        sc.op("pool", lambda e: e.memset(hh2[:, sl, :, 8:16], 0.0), writes=[("hh1", sl)])

                runE = stop not in ("A", "A2", "B", "C", "D")
                pend_steps = []
                if runE:
                    load_block(0)
                for b in (range(NB) if runE else ()):
                    bs = slice(b * TB, (b + 1) * TB)
                    sl = b % 2
                    if b + 1 < NB:
                        load_block(b + 1)
                    issue_late_cast()
                    sc.dma("act", hf[:], fm(hB)[:, :, bs], reads=[("hB", b)], writes=["hf"])
                    hb = hb2[:, sl]
                    hbh = hh2[:, sl, :, 7:9]
                    hbk = ("hb", sl)
                    hhk = [("hh0", sl), ("hh1", sl)]
                    sc.dma("act", pf[:], pT[l].rearrange("(k p) t -> p k t", p=128)[:, :, bs], writes=["pf"])
                    sc.op("pool", lambda e: e.tensor_copy(out=pb[:], in_=pf[:]), reads=["pf"], writes=["pb"])
                    n_a = 0
                    deferred = None
                    for i in range(NFF):
                        if i % 4 == 0:
                            wid = 512 if i + 4 <= NFF else 128 * (NFF - i)
                            s_v = wload("w_up", DFF + 128 * i, wid)
                            s_g = wload("w_up", 128 * i, wid)
                        o4 = (i % 4) * 128
                        jp = i % 2
                        for which in (0, 1):
                            s_w = s_v if which == 0 else s_g
                            ch = (NFF + i) if which == 0 else i
                            jm = n_a % 3
                            jh = (3, 4, 7)[n_a % 3]
                            n_a += 1
                            bk, bkey = B8[jm], "bank%d" % jm
                            bh, bhkey = B8[jh], "bank%d" % jh
                            for kc in range(KC):
                                sc.op("pe", lambda e, kc=kc: e.matmul(bk[:], wr[:, s_w, kc, o4:o4 + 128], hb[:, kc, :],
                                                                      start=(kc == 0), stop=(kc == KC - 1)),
                                      reads=[("wr", s_w), hbk], writes=[bkey])
                            for kc in range(KC):
                                sc.op("pe", lambda e, kc=kc: e.matmul(bh[:, 0:2], wr[:, s_w, kc, o4:o4 + 128], hbh[:, kc, :],
                                                                      start=(kc == 0), stop=(kc == KC - 1)),
                                      reads=[("wr", s_w)] + hhk, writes=[bhkey])
                            yo, yk = (yv, ("yv", jp)) if which == 0 else (yg, ("yg", jp))
                            w0, w1, w2, bb = (cw[:, l, q, ch:ch + 1] for q in range(4))
                            ja = (n_a - 1) % 3
                            ak = ("aext", ja)
                            sc.op("act", lambda e: e.activation(out=aext[:, ja, 1:513], in_=bk[:], func=AF.Copy), reads=[bkey], writes=[ak])
                            sc.op("act", lambda e: e.activation(out=aext[:, ja, 0:514:513], in_=bh[:, 0:2], func=AF.Copy), reads=[bhkey], writes=[ak])
                            if which == 0:
                                sc.op("act", lambda e: e.activation(out=yo[:, jp, :], in_=bk[:], func=AF.Identity, bias=bb, scale=w1),
                                      reads=[bkey, "cw"], writes=[yk])
                            else:
                                sc.op("dve", lambda e: e.tensor_scalar(out=yo[:, jp, :], in0=aext[:, ja, 1:513], scalar1=w1, scalar2=bb,
                                                                       op0=ALU.mult, op1=ALU.add),
                                      reads=[ak, "cw"], writes=[yk])
                            sc.op("dve", lambda e: e.scalar_tensor_tensor(out=yo[:, jp, :], in0=aext[:, ja, 0:512], scalar=w0,
                                                                          in1=yo[:, jp, :], op0=ALU.mult, op1=ALU.add),
                                  reads=[ak, "cw", yk], writes=[yk])
                            sc.op("dve", lambda e: e.scalar_tensor_tensor(out=yo[:, jp, :], in0=aext[:, ja, 2:514], scalar=w2,
                                                                          in1=yo[:, jp, :], op0=ALU.mult, op1=ALU.add),
                                  reads=[ak, "cw", yk], writes=[yk])
                            if deferred is not None and which == 0:
                                deferred()
                                deferred = None
                        def fin(i=i, jp=jp):
                            sc.op("act", lambda e: e.activation(out=yg[:, jp, :], in_=yg[:, jp, :], func=AF.Gelu),
                                  reads=[("yg", jp)], writes=[("yg", jp)])
                            sc.op("pool", lambda e: e.tensor_tensor(out=gT[:, i, :], in0=yg[:, jp, :], in1=yv[:, jp, :], op=ALU.mult),
                                  reads=[("yg", jp), ("yv", jp)], writes=[("gT", i)])
                        deferred = fin
                        if i >= 1 and i % 2 == 1 and pend_steps:
                            pend_steps.pop(0)()
                    deferred()
                    deferred = None
                    while pend_steps:
                        pend_steps.pop(0)()
                    n_o = 0
                    for half in range(2):
                        s_wd = []
                        for piece in range(3):
                            cnt = min(8, NFF - 8 * piece)
                            s_ = nring[0] % 6
                            nring[0] += 1
                            sc.dma("sp", wr[:, s_, 0:cnt, :], wdv[:, 8 * piece:8 * piece + cnt, 512 * half:512 * half + 512],
                                   reads=[("wb", "w_down", l)], writes=[("wr", s_)])
                            s_wd.append(s_)
                        s_pg = wload("w_ple_gate", 512 * half)
                        for o_ in range(4):
                            oc = 4 * half + o_
                            o4 = 128 * o_
                            ocs = slice(128 * oc, 128 * oc + 128)
                            st_ = n_o % 2
                            n_o += 1
                            ids = (0, 1, 2) if st_ == 0 else (3, 4, 7)
                            X0, X1, X2 = (B8[q] for q in ids)
                            k0, k1, k2_ = ("bank%d" % q for q in ids)
                            for i in range(NFF):
                                sc.op("pe", lambda e, i=i, sw=s_wd[i // 8]: e.matmul(X0[:], wr[:, sw, i % 8, o4:o4 + 128], gT[:, i, :],
                                                                                 start=(i == 0), stop=(i == NFF - 1)),
                                      reads=[("wr", s_wd[i // 8]), ("gT", i)], writes=[k0])
                            for k2 in range(2):
                                sc.op("pe", lambda e, k2=k2: e.matmul(X1[:], wple[:, k2, ocs], pb[:, k2, :], start=(k2 == 0), stop=(k2 == 1)),
                                      reads=["wple", "pb"], writes=[k1])
                            for kc in range(KC):
                                sc.op("pe", lambda e, kc=kc: e.matmul(X2[:], wr[:, s_pg, kc, o4:o4 + 128], hb[:, kc, :],
                                                                      start=(kc == 0), stop=(kc == KC - 1)),
                                      reads=[("wr", s_pg), hbk], writes=[k2_])
                            sc.op("act", lambda e: e.activation(out=sg[:, st_, :], in_=X2[:], func=AF.Sigmoid), reads=[k2_], writes=[("sg", st_)])
                            sc.op("dve", lambda e: e.tensor_tensor(out=sg[:, st_, :], in0=X1[:], in1=sg[:, st_, :], op=ALU.mult),
                                  reads=[k1, ("sg", st_)], writes=[("sg", st_)])
                            sc.op("dve", lambda e: e.scalar_tensor_tensor(out=r[:, oc, :], in0=hf[:, oc, :], scalar=ALPHA, in1=X0[:],
                                                                          op0=ALU.mult, op1=ALU.add),
                                  reads=["hf", k0], writes=[("r", oc)])
                            sc.op("dve", lambda e: e.tensor_tensor(out=r[:, oc, :], in0=r[:, oc, :], in1=sg[:, st_, :], op=ALU.add),
                                  reads=[("r", oc), ("sg", st_)], writes=[("r", oc)])
                            ln_acc(r, "r", oc, tmp)

                    if last:
                        outs, outs_b = [(fm(yT)[:, :, bs], ("yT", b))], []
                    else:
                        outs, outs_b = [(fm(hA)[:, :, bs], ("hA", b))], [(fm(hAb)[:, :, bs], ("hAb", b))]
                    if dump and l == 0:
                        outs.append((fm(dbg["d_h2"])[:, :, bs], ("d_h2", b)))

                    def done(toks, last=last):
                        if last:
                            final_tokens.append(toks[0])
                        if dump and l == 0:
                            final_tokens.append(toks[-1])
                    pend_steps = [lambda: ln_stats(r, "r", tmp)]
                    pend_steps += ln_apply_steps(r, "r", g2, b2, outs, outs_b, tmp, on_done=done,
                                                 cast_eng=("act" if b == NB - 1 else "pool"))
                while pend_steps:
                    pend_steps.pop(0)()
                sc.barrier()
            lay.close()
            while late_casts:
                issue_late_cast()

        sc.final_wait([(t[0], t[1], "fin") for t in final_tokens])
        with nc.Block() as block:
            sc.emit(block)
    return nc


def _consts():
    pos = np.arange(S, dtype=np.float32)
    inv_freq = (ROPE_THETA ** (-np.arange(0, 16, 2, dtype=np.float32) / 16.0)).astype(np.float32)
    ang = pos[None, :] * inv_freq[:, None]
    cos, sin = np.cos(ang).astype(np.float32), np.sin(ang).astype(np.float32)
    cosT = np.ones((128, S), np.float32)
    sinT = np.zeros((128, S), np.float32)
    rmat = np.zeros((128, 128), np.float32)
    for p in range(128):
        dd = p % 64
        if dd < 8:
            cosT[p] = cos[dd]
            sinT[p] = -sin[dd]
            rmat[p + 8, p] = 1.0
        elif dd < 16:
            cosT[p] = cos[dd - 8]
            sinT[p] = sin[dd - 8]
            rmat[p - 8, p] = 1.0
    ident = np.eye(128, dtype=np.float32)
    i = np.arange(128)[:, None]
    u = np.arange(128)[None, :]
    m1 = np.concatenate([(np.abs(128 * (t - 1) + i - u) <= 64).astype(np.float32) for t in range(3)], axis=1)
    mask3 = np.concatenate([m1, m1], axis=1)
    icnt = np.zeros((128, 2, 4, 8), np.float32)
    for gi in range(4):
        w = 2 << gi
        for c in range(8):
            for side, tok in ((0, c), (1, S - 8 + c)):
                lo, hi = max(tok - w // 2, 0), min(tok + w // 2, S)
                icnt[:, side, gi, c] = 1.0 / float(hi - lo)
    return dict(cosT=cosT, sinT=sinT, rmat=rmat, ident=ident, mask3=mask3, icnt=icnt)


def _vec(v):
    return np.ascontiguousarray(np.asarray(v, np.float32).reshape(-1, 128).T)


def make_in_maps(inputs, n_cores=8):
    x = np.asarray(inputs["x"], np.float32)
    p = np.asarray(inputs["p"], np.float32)
    shared = dict(_consts())
    for (n, k, c) in WEIGHTS:
        shared[n] = np.ascontiguousarray(np.asarray(inputs[n], np.float32).reshape(-1, 1024))
    lnv = np.zeros((128, 2 + 4 * L, KC), np.float32)
    lnv[:, 0] = _vec(inputs["ln0_g"])
    lnv[:, 1] = _vec(inputs["ln0_b"])
    for l in range(L):
        lnv[:, 2 + 4 * l] = _vec(inputs["ln1_g"][l])
        lnv[:, 3 + 4 * l] = _vec(inputs["ln1_b"][l])
        lnv[:, 4 + 4 * l] = _vec(inputs["ln2_g"][l])
        lnv[:, 5 + 4 * l] = _vec(inputs["ln2_b"][l])
    shared["lnv"] = lnv
    psc = np.zeros((128, L, 4), np.float32)
    cwv = np.zeros((128, L, 4, 2 * NFF), np.float32)
    for l in range(L):
        psc[:, l] = _vec(inputs["pool_scale"][l])
        for j in range(3):
            cwv[:, l, j] = _vec(inputs["conv_w"][l][j])
        cwv[:, l, 3] = _vec(inputs["conv_b"][l])
    shared["pscale"] = psc
    shared["convw"] = cwv
    maps = []
    for c in range(n_cores):
        m = dict(shared)
        m["xT"] = np.ascontiguousarray(x[c].T)
        m["pT"] = np.ascontiguousarray(np.transpose(p[:, c], (0, 2, 1)))
        maps.append(m)
    return maps


def kernel(**inputs):
    nc = build_program()
    maps = make_in_maps(inputs, 8)
    res = run_bass_kernel_spmd(nc, maps, core_ids=list(range(8)))
    out = np.stack([np.ascontiguousarray(np.asarray(r["yT"], np.float32).T) for r in res.results], axis=0)
    return out.astype(np.float32)
```
